# Optimizing a Trainium2 kernel written in Bass

```python
import math
import jax, jax.numpy as jnp
from jax import lax
import numpy as np

D_MODEL = 1024
BATCH = 4
SEQ = 4096
DEPTH = 4

kernel_name = 'hybrid_ssd_dilated_diffattn_encoder'

N_MIXERS = 3
N_SSD_LAYERS = (DEPTH + 2) // 3
N_DIL_LAYERS = (DEPTH + 1) // 3
N_DIFF_LAYERS = DEPTH // 3

RMS_EPS = 1e-6
D_FF = 4 * D_MODEL

SSD_EXPAND = 2
SSD_D_INNER = SSD_EXPAND * D_MODEL
SSD_HEAD_DIM = 64
SSD_HEADS = SSD_D_INNER // SSD_HEAD_DIM
SSD_GROUPS = 8
SSD_HEADS_PER_GROUP = SSD_HEADS // SSD_GROUPS
SSD_D_STATE = 128
SSD_CONV_WIDTH = 5
SSD_CHUNK = 128
SSD_CONV_CH = SSD_D_INNER + 2 * SSD_GROUPS * SSD_D_STATE
SSD_IN_COLS = SSD_D_INNER + SSD_CONV_CH + 2 * SSD_HEADS

DIL_CONFIGS = ((128, 1), (512, 4), (2048, 16))
DIL_GROUPS = len(DIL_CONFIGS)
DIL_HEADS = 16
DIL_HEAD_DIM = D_MODEL // DIL_HEADS
DIL_WIDTH = DIL_HEADS * DIL_HEAD_DIM
DIL_QKV_COLS = 3 * DIL_GROUPS * DIL_WIDTH

DIFF_HEADS = 8
DIFF_HEAD_DIM = D_MODEL // (2 * DIFF_HEADS)
DIFF_V_DIM = 2 * DIFF_HEAD_DIM
DIFF_QKV_COLS = 3 * D_MODEL
DIFF_QBLK = 128

REL_BUCKETS = 32
REL_MAX_DIST = 1024
REL_BIAS_HEADS = 16
NEG_INF = -1e30


def rmsnorm(x, g):
    xf = x.astype(jnp.float32)
    y = xf * lax.rsqrt(jnp.mean(xf * xf, axis=-1, keepdims=True) + RMS_EPS)
    return (y * g.astype(jnp.float32)).astype(x.dtype)


def rel_bucket(rel):
    half = REL_BUCKETS // 2
    max_exact = half // 2
    ret = jnp.where(rel > 0, half, 0)
    n = jnp.abs(rel)
    nf = jnp.maximum(n, 1).astype(jnp.float32)
    large = max_exact + (jnp.log(nf / max_exact) / math.log(REL_MAX_DIST / max_exact)
                         * (half - max_exact)).astype(jnp.int32)
    large = jnp.minimum(large, half - 1)
    return ret + jnp.where(n < max_exact, n, large)


def rel_bias_lookup(table, rel):
    return table.astype(jnp.float32)[rel_bucket(rel)]


def ssd_chunk_scan(x, dt, a, b_in, c_in):
    bsz, s, g, r, p = x.shape
    n = b_in.shape[-1]
    L = SSD_CHUNK
    nc = s // L
    xdt = x.astype(jnp.float32) * dt[..., None]
    adt = dt * a

    def chunks(t):
        return jnp.swapaxes(t.reshape((bsz, nc, L) + t.shape[2:]), 0, 1)

    tri = jnp.tril(jnp.ones((L, L), dtype=bool))[None, :, :, None, None]

    def step(state, inp):
        xc, ac, bc, cc = inp
        acs = jnp.cumsum(ac, axis=1)
        seg = acs[:, :, None] - acs[:, None, :]
        lmat = jnp.exp(jnp.where(tri, seg, -jnp.inf))
        cb = jnp.einsum('blgn,bsgn->blsg', cc, bc)
        y_diag = jnp.einsum('blsgr,bsgrp->blgrp', cb[..., None] * lmat, xc)
        y_off = jnp.einsum('blgn,bgrpn->blgrp', cc, state) * jnp.exp(acs)[..., None]
        decay = jnp.exp(acs[:, -1:] - acs)
        new_state = (state * jnp.exp(acs[:, -1])[..., None, None]
                     + jnp.einsum('bsgn,bsgr,bsgrp->bgrpn', bc, decay, xc))
        return new_state, y_diag + y_off

    state0 = jnp.zeros((bsz, g, r, p, n), jnp.float32)
    _, ys = lax.scan(step, state0, (chunks(xdt), chunks(adt),
                                    chunks(b_in.astype(jnp.float32)),
                                    chunks(c_in.astype(jnp.float32))))
    return jnp.swapaxes(ys, 0, 1).reshape(bsz, s, g, r, p)


def ssd_mixer(h, in_w, conv_w, conv_b, dt_bias, a_log, d_skip, norm_g, out_w):
    bsz, s, _ = h.shape
    G, R, P, N = SSD_GROUPS, SSD_HEADS_PER_GROUP, SSD_HEAD_DIM, SSD_D_STATE
    zxbcdt = h @ in_w
    z, xbc, dt = jnp.split(zxbcdt, [SSD_D_INNER, SSD_D_INNER + SSD_CONV_CH], axis=-1)
    pad = SSD_CONV_WIDTH // 2
    xbc = lax.conv_general_dilated(xbc, conv_w[:, None, :], (1,), [(pad, pad)],
                                   dimension_numbers=('NWC', 'WIO', 'NWC'),
                                   feature_group_count=SSD_CONV_CH)
    xbc = jax.nn.silu(xbc + conv_b)
    xs, bs, cs = jnp.split(xbc, [SSD_D_INNER, SSD_D_INNER + G * N], axis=-1)
    xs = xs.reshape(bsz, s, G, R, P)
    bs = bs.reshape(bsz, s, G, N)
    cs = cs.reshape(bsz, s, G, N)
    dt = jax.nn.softplus(dt.astype(jnp.float32).reshape(bsz, s, 2, G, R)
                         + dt_bias.astype(jnp.float32).reshape(2, G, R))
    a = -jnp.exp(a_log.astype(jnp.float32)).reshape(2, G, R)
    flip = lambda t: jnp.flip(t, axis=1)
    y_fwd = ssd_chunk_scan(xs, dt[:, :, 0], a[0], bs, cs)
    y_bwd = flip(ssd_chunk_scan(flip(xs), flip(dt[:, :, 1]), a[1], flip(bs), flip(cs)))
    y = y_fwd + y_bwd + d_skip.astype(jnp.float32).reshape(G, R)[..., None] * xs.astype(jnp.float32)
    y = y.reshape(bsz, s, SSD_D_INNER) * jax.nn.silu(z.astype(jnp.float32))
    y = rmsnorm(y, norm_g).astype(h.dtype)
    return y @ out_w


def dilated_group(q, k, v, window, dilation, table):
    bsz, s, nh, hd = q.shape
    d = dilation
    n = s // d
    w = window // (2 * d)
    blk = w
    nb = -(-n // blk)
    n_p = nb * blk

    def to_sub(t):
        return t.reshape(bsz, n, d, nh, hd).transpose(0, 2, 1, 3, 4).astype(jnp.float32)

    qs = jnp.pad(to_sub(q), ((0, 0), (0, 0), (0, n_p - n), (0, 0), (0, 0)))
    qs = qs.reshape(bsz, d, nb, blk, nh, hd)

    def band(t):
        tp = jnp.pad(to_sub(t), ((0, 0), (0, 0), (blk, n_p - n + blk), (0, 0), (0, 0)))
        tp = tp.reshape(bsz, d, nb + 2, blk, nh, hd)
        return jnp.concatenate([tp[:, :, :-2], tp[:, :, 1:-1], tp[:, :, 2:]], axis=3)

    kb, vb = band(k), band(v)
    scores = jnp.einsum('brjqhe,brjkhe->brjhqk', qs, kb) / math.sqrt(hd)
    delta = jnp.arange(3 * blk)[None, :] - blk - jnp.arange(blk)[:, None]
    key_idx = jnp.arange(nb)[:, None] * blk - blk + jnp.arange(3 * blk)[None, :]
    valid = (jnp.abs(delta) <= w)[None] & ((key_idx >= 0) & (key_idx < n))[:, None, :]
    bias = jnp.moveaxis(rel_bias_lookup(table, delta * d), -1, 0)
    scores = jnp.where(valid[:, None], scores + bias, NEG_INF)
    m = jnp.max(scores, axis=-1, keepdims=True)
    p = jnp.exp(scores - m)
    den = jnp.sum(p, axis=-1, keepdims=True)
    o = jnp.einsum('brjhqk,brjkhe->brjqhe', p, vb) / jnp.swapaxes(den, 3, 4)
    lse = jnp.swapaxes((m + jnp.log(den))[..., 0], 3, 4)
    o = o.reshape(bsz, d, n_p, nh, hd)[:, :, :n].transpose(0, 2, 1, 3, 4).reshape(bsz, s, nh, hd)
    lse = lse.reshape(bsz, d, n_p, nh)[:, :, :n].transpose(0, 2, 1, 3).reshape(bsz, s, nh)
    return o, lse


def dilated_mixer(h, qkv_w, out_w, table):
    bsz, s, _ = h.shape
    qkv = (h @ qkv_w).reshape(bsz, s, 3, DIL_GROUPS, DIL_HEADS, DIL_HEAD_DIM)
    outs, lses = [], []
    for gi, (win, dil) in enumerate(DIL_CONFIGS):
        o, l = dilated_group(qkv[:, :, 0, gi], qkv[:, :, 1, gi], qkv[:, :, 2, gi], win, dil, table)
        outs.append(o)
        lses.append(l)
    wts = jax.nn.softmax(jnp.stack(lses), axis=0)
    o = jnp.sum(wts[..., None] * jnp.stack(outs), axis=0)
    return o.reshape(bsz, s, DIL_WIDTH).astype(h.dtype) @ out_w


def diff_mixer(h, qkv_w, lam, subln_g, out_w, table, layer_idx):
    bsz, s, _ = h.shape
    H, E, V = DIFF_HEADS, DIFF_HEAD_DIM, DIFF_V_DIM
    q, k, v = jnp.split(h @ qkv_w, 3, axis=-1)
    q = q.reshape(bsz, s, H, 2, E).astype(jnp.float32) / math.sqrt(E)
    k = k.reshape(bsz, s, H, 2, E).astype(jnp.float32)
    v = v.reshape(bsz, s, H, V).astype(jnp.float32)
    lam_init = 0.8 - 0.6 * math.exp(-0.3 * layer_idx)
    lf = lam.astype(jnp.float32)
    lam_full = jnp.exp(jnp.sum(lf[0] * lf[1])) - jnp.exp(jnp.sum(lf[2] * lf[3])) + lam_init
    nq = s // DIFF_QBLK
    q_blocks = jnp.swapaxes(q.reshape(bsz, nq, DIFF_QBLK, H, 2, E), 0, 1)
    starts = jnp.arange(nq) * DIFF_QBLK
    key_pos = jnp.arange(s)

    def block(args):
        qb, st = args
        rel = key_pos[None, :] - (st + jnp.arange(DIFF_QBLK))[:, None]
        bias = rel_bias_lookup(table, rel).reshape(DIFF_QBLK, s, H, 2).transpose(2, 3, 0, 1)
        logits = jnp.einsum('bqhie,bkhie->bhiqk', qb, k) + bias
        a = jax.nn.softmax(logits, axis=-1)
        attn = a[:, :, 0] - lam_full * a[:, :, 1]
        return jnp.einsum('bhqk,bkhe->bqhe', attn, v)

    o = lax.map(block, (q_blocks, starts))
    o = jnp.swapaxes(o, 0, 1).reshape(bsz, s, H, V)
    o = rmsnorm(o, subln_g) * (1.0 - lam_init)
    return o.reshape(bsz, s, D_MODEL).astype(h.dtype) @ out_w


def sqrelu_mlp(h, w1, w2):
    return jnp.square(jax.nn.relu(h @ w1)) @ w2


def setup_inputs(seed: int = 0) -> dict:
    key = jax.random.key(seed)
    ks = jax.random.split(key, 24)
    nrm = lambda k, shape, scale: jax.random.normal(k, shape, jnp.float32) * scale
    dt0 = jnp.exp(jax.random.uniform(ks[9], (N_SSD_LAYERS, 2, SSD_HEADS), jnp.float32,
                                     minval=math.log(1e-3), maxval=math.log(1e-1)))
    return {
        'x': nrm(ks[0], (BATCH, SEQ, D_MODEL), 1.0),
        'rel_bias': nrm(ks[1], (REL_BUCKETS, REL_BIAS_HEADS), 0.2),
        'norm_mix_g': 1.0 + nrm(ks[2], (DEPTH, D_MODEL), 0.02),
        'norm_mlp_g': 1.0 + nrm(ks[3], (DEPTH, D_MODEL), 0.02),
        'mlp_w1': nrm(ks[4], (DEPTH, D_MODEL, D_FF), D_MODEL ** -0.5),
        'mlp_w2': nrm(ks[5], (DEPTH, D_FF, D_MODEL), D_FF ** -0.5),
        'ssd_in_w': nrm(ks[6], (N_SSD_LAYERS, D_MODEL, SSD_IN_COLS), D_MODEL ** -0.5),
        'ssd_conv_w': nrm(ks[7], (N_SSD_LAYERS, SSD_CONV_WIDTH, SSD_CONV_CH), SSD_CONV_WIDTH ** -0.5),
        'ssd_conv_b': nrm(ks[8], (N_SSD_LAYERS, SSD_CONV_CH), 0.02),
        'ssd_dt_bias': dt0 + jnp.log(-jnp.expm1(-dt0)),
        'ssd_a_log': jnp.log(jax.random.uniform(ks[10], (N_SSD_LAYERS, 2, SSD_HEADS), jnp.float32,
                                                minval=1.0, maxval=16.0)),
        'ssd_d': 1.0 + nrm(ks[11], (N_SSD_LAYERS, SSD_HEADS), 0.1),
        'ssd_norm_g': 1.0 + nrm(ks[12], (N_SSD_LAYERS, SSD_D_INNER), 0.02),
        'ssd_out_w': nrm(ks[13], (N_SSD_LAYERS, SSD_D_INNER, D_MODEL), SSD_D_INNER ** -0.5),
        'dil_qkv_w': nrm(ks[14], (N_DIL_LAYERS, D_MODEL, DIL_QKV_COLS), D_MODEL ** -0.5),
        'dil_out_w': nrm(ks[15], (N_DIL_LAYERS, DIL_WIDTH, D_MODEL), DIL_WIDTH ** -0.5),
        'diff_qkv_w': nrm(ks[16], (N_DIFF_LAYERS, D_MODEL, DIFF_QKV_COLS), D_MODEL ** -0.5),
        'diff_lambda': nrm(ks[17], (N_DIFF_LAYERS, 4, DIFF_HEAD_DIM), 0.1),
        'diff_subln_g': 1.0 + nrm(ks[18], (N_DIFF_LAYERS, DIFF_V_DIM), 0.02),
        'diff_out_w': nrm(ks[19], (N_DIFF_LAYERS, D_MODEL, D_MODEL), D_MODEL ** -0.5),
        'final_norm_g': 1.0 + nrm(ks[20], (D_MODEL,), 0.02),
    }


def reference(x, rel_bias, norm_mix_g, norm_mlp_g, mlp_w1, mlp_w2,
              ssd_in_w, ssd_conv_w, ssd_conv_b, ssd_dt_bias, ssd_a_log, ssd_d,
              ssd_norm_g, ssd_out_w, dil_qkv_w, dil_out_w,
              diff_qkv_w, diff_lambda, diff_subln_g, diff_out_w, final_norm_g):
    for i in range(DEPTH):
        kind = i % N_MIXERS
        j = i // N_MIXERS
        h = rmsnorm(x, norm_mix_g[i])
        if kind == 0:
            h = ssd_mixer(h, ssd_in_w[j], ssd_conv_w[j], ssd_conv_b[j], ssd_dt_bias[j],
                          ssd_a_log[j], ssd_d[j], ssd_norm_g[j], ssd_out_w[j])
        elif kind == 1:
            h = dilated_mixer(h, dil_qkv_w[j], dil_out_w[j], rel_bias)
        else:
            h = diff_mixer(h, diff_qkv_w[j], diff_lambda[j], diff_subln_g[j], diff_out_w[j],
                           rel_bias, i)
        x = x + h
        x = x + sqrelu_mlp(rmsnorm(x, norm_mlp_g[i]), mlp_w1[i], mlp_w2[i])
    return rmsnorm(x, final_norm_g)
```

```python
import contextlib
import numpy as np
import concourse.bass as bass
import concourse.mybir as mybir
from concourse.bass_utils import run_bass_kernel_spmd

F32 = mybir.dt.float32
BF16 = mybir.dt.bfloat16
AF = mybir.ActivationFunctionType
ALU = mybir.AluOpType
AX = mybir.AxisListType

ENGS = ("pe", "dve", "act", "pool", "sp")
S_LEN = 4096
D = 1024
NT = S_LEN // 128
EPS = 1e-6


class Sched:
    def __init__(self, nc, strict_same_engine=True):
        self.nc = nc
        self.strict = strict_same_engine
        self.ops = {e: [] for e in ENGS}
        self.res = {}
        self.dma_cnt = {}
        self.dma_last = {}
        self.stack = contextlib.ExitStack()
        self.nops = 0

    def sbuf(self, name, shape, dtype):
        return self.stack.enter_context(self.nc.sbuf_tensor(name, list(shape), dtype))

    def psum(self, name, shape, dtype=F32):
        return self.stack.enter_context(self.nc.psum_tensor(name, list(shape), dtype))

    @staticmethod
    def _ekey(ev):
        return (ev[0], ev[1])

    def _deps(self, reads, writes):
        waits = []
        for k in reads:
            r = self.res.get(k)
            if r:
                waits.extend(r["w"].values())
        for k in writes:
            r = self.res.get(k)
            if r:
                waits.extend(r["w"].values())
                waits.extend(r["r"].values())
        return waits

    def _commit(self, ev, reads, writes, pwrites=()):
        ek = self._ekey(ev)
        for k in reads:
            r = self.res.setdefault(k, {"w": {}, "r": {}})
            r["r"][ek] = ev
        for k in writes:
            self.res[k] = {"w": {ek: ev}, "r": {}}
        for k in pwrites:
            r = self.res.setdefault(k, {"w": {}, "r": {}})
            r["w"][ek] = ev

    def op(self, eng, fn, reads=(), writes=(), pwrites=()):
        reads = tuple(reads)
        writes = tuple(writes)
        pwrites = tuple(pwrites)
        waits = self._deps(reads, writes + pwrites)
        idx = len(self.ops[eng])
        ev = ("e", eng, idx)
        self.ops[eng].append({"fn": fn, "waits": waits, "ev": ev, "dma": None})
        self._commit(ev, reads, writes, pwrites)
        self.nops += 1
        return ev

    def dma(self, q, chan, fn, reads=(), writes=(), pwrites=()):
        reads = tuple(reads)
        writes = tuple(writes)
        pwrites = tuple(pwrites)
        waits = self._deps(reads, writes + pwrites)
        n = self.dma_cnt.get(chan, 0) + 1
        self.dma_cnt[chan] = n
        if chan in self.dma_last:
            waits.append(self.dma_last[chan])
        ev = ("d", chan, n)
        self.dma_last[chan] = ev
        self.ops[q].append({"fn": fn, "waits": waits, "ev": ev, "dma": chan})
        self._commit(ev, reads, writes, pwrites)
        self.nops += 1
        return ev

    def emit(self):
        nc = self.nc
        needed = {e: set() for e in ENGS}
        for e in ENGS:
            for o in self.ops[e]:
                for w in o["waits"]:
                    if w[0] == "e":
                        if w[1] == e and (e == "pe" or not self.strict):
                            continue
                        needed[w[1]].add(w[2])
        seq = {}
        for e in ENGS:
            c = 0
            for i, o in enumerate(self.ops[e]):
                if o["dma"] is None and i in needed[e]:
                    c += 1
                    seq[(e, i)] = c
        self.maxseq = {e: max([v for k, v in seq.items() if k[0] == e], default=0) for e in ENGS}
        sems = {}
        for e in ENGS:
            sems[("e", e)] = self.stack.enter_context(nc.semaphore("s_" + e))
        for ch in self.dma_cnt:
            sems[("d", ch)] = self.stack.enter_context(nc.semaphore("d_" + str(ch)))
        engobj = {"pe": "tensor", "dve": "vector", "act": "scalar", "pool": "gpsimd", "sp": "sync"}
        last_dma = list(self.dma_last.values())

        def run(e, engine):
            known = {}
            for i, o in enumerate(self.ops[e]):
                need = {}
                for w in o["waits"]:
                    if w[0] == "e":
                        if w[1] == e and (e == "pe" or not self.strict):
                            continue
                        key = ("e", w[1])
                        val = seq[(w[1], w[2])]
                    else:
                        key = ("d", w[1])
                        val = 16 * w[2]
                    if val > need.get(key, 0):
                        need[key] = val
                for key, val in need.items():
                    if known.get(key, 0) >= val:
                        continue
                    engine.wait_ge(sems[key], val)
                    known[key] = val
                ins = o["fn"](engine)
                if o["dma"] is not None:
                    ins.then_inc(sems[("d", o["dma"])], 16)
                elif (e, i) in seq:
                    ins.then_inc(sems[("e", e)], 1)
            if e == "sp":
                for ev in last_dma:
                    engine.wait_ge(sems[("d", ev[1])], 16 * ev[2])

        with nc.Block() as block:
            for e in ENGS:
                if not self.ops[e] and e != "sp":
                    continue
                getattr(block, engobj[e])(lambda engine, e=e: run(e, engine))

    def close(self):
        self.stack.close()


class Prologue:
    def __init__(self, S, nc, mode, banks, bank_keys):
        self.S, self.nc, self.mode = S, nc, mode
        self.x = nc.dram_tensor("x", [S_LEN, D], F32, kind="ExternalInput").ap()
        self.ident_d = nc.dram_tensor("ident", [128, 128], F32, kind="ExternalInput").ap()
        self.g_d = nc.dram_tensor("gT", [128, 8], F32, kind="ExternalInput").ap()
        if mode != "first":
            self.pa = nc.dram_tensor("pa", [S_LEN, D], F32, kind="ExternalInput").ap()
            self.pb = nc.dram_tensor("pb", [S_LEN, D], F32, kind="ExternalInput").ap()
            self.xo = nc.dram_tensor("xo", [S_LEN, D], F32, kind="ExternalOutput").ap()
        if mode == "ssd":
            self.ssa = nc.dram_tensor("ssa", [128, NT], F32, kind="ExternalInput").ap()
            self.ssb = nc.dram_tensor("ssb", [128, NT], F32, kind="ExternalInput").ap()
        self.ident = S.sbuf("ident_sb", [128, 128], F32)
        self.gT = S.sbuf("gT_sb", [128, 8], F32)
        self.xt = [S.sbuf(f"xt{i}", [128, D], F32) for i in range(2)]
        if mode != "first":
            self.pat = [S.sbuf(f"pat{i}", [128, D], F32) for i in range(2)]
            self.pbt = [S.sbuf(f"pbt{i}", [128, D], F32) for i in range(2)]
        else:
            self.pat = [S.sbuf(f"pat{i}", [128, D], F32) for i in range(2)]
        self.st = [S.sbuf(f"stat{i}", [128, 4], F32) for i in range(2)]
        self.tp = list(banks)
        self.tpk = list(bank_keys)
        if mode == "ssd":
            self.rs_in = S.sbuf("rs_in", [128, NT], F32)
            self.rs_b = S.sbuf("rs_b", [128, NT], F32)
        S.dma("sp", "c0", lambda e: e.dma_start(out=self.ident[:], in_=self.ident_d), writes=["ident"])
        S.dma("sp", "c0", lambda e: e.dma_start(out=self.gT[:], in_=self.g_d), writes=["gT"])
        if mode == "ssd":
            S.dma("sp", "c0", lambda e: e.dma_start(out=self.rs_in[:], in_=self.ssa), writes=["rs_in"])
            S.dma("sp", "c0", lambda e: e.dma_start(out=self.rs_b[:], in_=self.ssb), writes=["rs_b"])
            S.op("dve", lambda e: e.tensor_tensor(out=self.rs_in[:], in0=self.rs_in[:], in1=self.rs_b[:], op=ALU.add),
                 reads=["rs_in", "rs_b"], writes=["rs_in"])
            S.op("dve", lambda e: e.tensor_scalar(out=self.rs_in[:], in0=self.rs_in[:], scalar1=1.0 / 2048.0,
                                                  scalar2=EPS, op0=ALU.mult, op1=ALU.add),
                 reads=["rs_in"], writes=["rs_in"])
            S.op("act", lambda e: e.activation(out=self.rs_in[:], in_=self.rs_in[:], func=AF.Sqrt),
                 reads=["rs_in"], writes=["rs_in"])
            S.op("dve", lambda e: e.reciprocal(out=self.rs_in[:], in_=self.rs_in[:]),
                 reads=["rs_in"], writes=["rs_in"])

    def tile(self, t, hT_ap_fn, hT_key, want_h=True):
        S, mode = self.S, self.mode
        sl = t % 2
        xt, pat, st = self.xt[sl], self.pat[sl], self.st[sl]
        rows = slice(t * 128, (t + 1) * 128)
        kx, kp, kb, ks = ("xt", sl), ("pat", sl), ("pbt", sl), ("st", sl)
        S.dma("sp", "ldx", lambda e: e.dma_start(out=xt[:], in_=self.x[rows, :]), writes=[kx])
        if mode != "first":
            pbt = self.pbt[sl]
            S.dma("sp", "lda", lambda e: e.dma_start(out=pat[:], in_=self.pa[rows, :]), writes=[kp])
            S.dma("sp", "ldb", lambda e: e.dma_start(out=pbt[:], in_=self.pb[rows, :]), writes=[kb])
            S.op("pool", lambda e: e.tensor_tensor(out=pat[:], in0=pat[:], in1=pbt[:], op=ALU.add),
                 reads=[kp, kb], writes=[kp])
            if mode == "ssd":
                S.op("dve", lambda e: e.scalar_tensor_tensor(out=xt[:], in0=pat[:], scalar=self.rs_in[:, t:t + 1],
                                                             in1=xt[:], op0=ALU.mult, op1=ALU.add),
                     reads=[kp, kx, "rs_in"], writes=[kx])
            else:
                S.op("dve", lambda e: e.tensor_tensor(out=xt[:], in0=xt[:], in1=pat[:], op=ALU.add),
                     reads=[kp, kx], writes=[kx])
            S.dma("act", "stx", lambda e: e.dma_start(out=self.xo[rows, :], in_=xt[:]), reads=[kx])
        if not want_h:
            return
        S.op("act", lambda e: e.activation(out=pat[:], in_=xt[:], func=AF.Square, accum_out=st[:, 0:1]),
             reads=[kx], writes=[kp, ks])
        S.op("dve", lambda e: e.tensor_scalar(out=st[:, 1:2], in0=st[:, 0:1], scalar1=1.0 / D, scalar2=EPS,
                                              op0=ALU.mult, op1=ALU.add), reads=[ks], writes=[ks])
        S.op("act", lambda e: e.activation(out=st[:, 2:3], in_=st[:, 1:2], func=AF.Sqrt), reads=[ks], writes=[ks])
        S.op("dve", lambda e: e.reciprocal(out=st[:, 3:4], in_=st[:, 2:3]), reads=[ks], writes=[ks])
        S.op("act", lambda e: e.activation(out=pat[:], in_=xt[:], func=AF.Copy, scale=st[:, 3:4]),
             reads=[kx, ks], writes=[kp])
        for hb in range(2):
            tp = self.tp[hb]
            ktp = self.tpk[hb]

            def f_tr(e, tp=tp, hb=hb):
                ins = None
                for j in range(4):
                    kt = hb * 4 + j
                    ins = e.transpose(out=tp[:, j * 128:(j + 1) * 128], in_=pat[:, kt * 128:(kt + 1) * 128],
                                      identity=self.ident[:])
                return ins
            S.op("pe", f_tr, reads=[kp, "ident"], writes=[ktp])
            dst = hT_ap_fn(hb * 4, 4)
            gb = self.gT[:, hb * 4:hb * 4 + 4].unsqueeze(2).to_broadcast([128, 4, 128])
            S.op("dve", lambda e, tp=tp, dst=dst, gb=gb: e.tensor_tensor(
                out=dst, in0=tp[:].rearrange("p (k n) -> p k n", k=4), in1=gb, op=ALU.mult),
                reads=[ktp, "gT"], writes=[hT_key])


def cast_load(S, q, chan, dst_ap, src_ap, key):
    S.dma(q, chan, lambda e: e.dma_start(out=dst_ap, in_=src_ap, max_dma_last_dim=4096), writes=[key])


def build_mlp(mode):
    nc = bass.Bass("TRN2", target_bir_lowering=False)
    S = Sched(nc)
    bk = [S.psum(f"bk{i}", [128, 512], F32) for i in range(8)]
    P = Prologue(S, nc, mode, bk[6:8], [("bk", 6), ("bk", 7)])
    w1 = nc.dram_tensor("w1", [D, 2048], F32, kind="ExternalInput").ap()
    w2 = nc.dram_tensor("w2", [2048, D], F32, kind="ExternalInput").ap()
    po = nc.dram_tensor("po", [S_LEN, D], F32, kind="ExternalOutput").ap()
    W1 = S.sbuf("W1", [128, 8, 2048], BF16)
    W2 = S.sbuf("W2", [128, 16, 1024], BF16)
    hT = [S.sbuf(f"hT{i}", [128, 8, 512], BF16) for i in range(2)]
    uT = [S.sbuf(f"uT{i}", [128, 16, 512], BF16) for i in range(2)]
    rT = [S.sbuf(f"rT{i}", [128, 512], F32) for i in range(2)]
    pot = [S.sbuf(f"pot{i}", [128, D], F32) for i in range(2)]
    ph = bk[0:2]
    pq = bk[2:4]
    for kt in range(8):
        cast_load(S, "pool", "w", W1[:, kt, :], w1[kt * 128:(kt + 1) * 128, :], ("W1", kt))
    for ft in range(16):
        cast_load(S, "pool", "w", W2[:, ft, :], w2[ft * 128:(ft + 1) * 128, :], ("W2", ft))
    nmm = 0
    for c in range(S_LEN // 512):
        hs = c % 2
        for tt in range(4):
            P.tile(c * 4 + tt,
                   lambda k0, n, tt=tt, hs=hs: hT[hs][:, k0:k0 + n, tt * 128:(tt + 1) * 128],
                   ("hT", hs))
        us = c % 2
        for ft in range(16):
            b = nmm % 2
            nmm += 1

            def f_mm(e, ft=ft, b=b, hs=hs):
                ins = None
                for kt in range(8):
                    ins = e.matmul(out=ph[b][:], lhsT=W1[:, kt, ft * 128:(ft + 1) * 128], rhs=hT[hs][:, kt, :],
                                   start=(kt == 0), stop=(kt == 7))
                return ins
            S.op("pe", f_mm, reads=[("hT", hs)] + [("W1", kt) for kt in range(8)], writes=[("ph", b)])
            S.op("act", lambda e, b=b: e.activation(out=rT[b][:], in_=ph[b][:], func=AF.Relu),
                 reads=[("ph", b)], writes=[("rT", b)])
            S.op("dve", lambda e, b=b, ft=ft, us=us: e.tensor_tensor(out=uT[us][:, ft, :], in0=ph[b][:], in1=rT[b][:],
                                                                      op=ALU.mult),
                 reads=[("ph", b), ("rT", b)], writes=[("uT", us, ft)])
        for tt in range(4):
            t = c * 4 + tt
            ps_ = t % 2
            for nh in range(2):
                b = nh

                def f_mm2(e, tt=tt, nh=nh, us=us, b=b):
                    ins = None
                    for ft in range(16):
                        ins = e.matmul(out=pq[b][:], lhsT=uT[us][:, ft, tt * 128:(tt + 1) * 128],
                                       rhs=W2[:, ft, nh * 512:(nh + 1) * 512], start=(ft == 0), stop=(ft == 15))
                    return ins
                S.op("pe", f_mm2, reads=[("uT", us, ft) for ft in range(16)] + [("W2", ft) for ft in range(16)],
                     writes=[("pq", b)])
                S.op("act", lambda e, b=b, ps_=ps_, nh=nh: e.activation(out=pot[ps_][:, nh * 512:(nh + 1) * 512],
                                                                        in_=pq[b][:], func=AF.Copy),
                     reads=[("pq", b)], writes=[("pot", ps_, nh)])
            S.dma("act", "sto", lambda e, t=t, ps_=ps_: e.dma_start(out=po[t * 128:(t + 1) * 128, :], in_=pot[ps_][:]),
                  reads=[("pot", ps_, 0), ("pot", ps_, 1)])
    S.emit()
    S.close()
    return nc, S


DIFF_STRIP = 2176


def build_diff(layer_idx, mode="add"):
    import math
    lam_init = 0.8 - 0.6 * math.exp(-0.3 * layer_idx)
    nc = bass.Bass("TRN2", target_bir_lowering=False)
    S = Sched(nc)
    bk = [S.psum(f"bk{i}", [128, 512], F32) for i in range(8)]
    BK = lambda i: ("bk", i)
    P = Prologue(S, nc, mode, bk[6:8], [BK(6), BK(7)])
    wq = nc.dram_tensor("wq", [D, 512], F32, kind="ExternalInput").ap()
    wk = nc.dram_tensor("wk", [D, 512], F32, kind="ExternalInput").ap()
    wv = nc.dram_tensor("wv", [D, 512], F32, kind="ExternalInput").ap()
    wo = nc.dram_tensor("wo", [512, D], F32, kind="ExternalInput").ap()
    lam_d = nc.dram_tensor("lam", [128, 256], F32, kind="ExternalInput").ap()
    sg_d = nc.dram_tensor("sg", [128, 128], F32, kind="ExternalInput").ap()
    tfar_d = nc.dram_tensor("tfar", [128, 16], F32, kind="ExternalInput").ap()
    strips_d = nc.dram_tensor("strips", [8, 128, DIFF_STRIP], F32, kind="ExternalInput").ap()
    po = nc.dram_tensor("po", [S_LEN, D], F32, kind="ExternalOutput").ap()
    QTd = nc.dram_tensor("QTd", [4, 128, S_LEN], BF16).ap()
    KTd = nc.dram_tensor("KTd", [4, 128, S_LEN], BF16).ap()
    Vd = nc.dram_tensor("Vd", [4, 128, NT * 130], BF16).ap()
    ONd = nc.dram_tensor("ONd", [S_LEN, 512], F32).ap()

    Wq = S.sbuf("Wq", [128, 8, 512], BF16)
    Wk = S.sbuf("Wk", [128, 8, 512], BF16)
    Wv = S.sbuf("Wv", [128, 8, 512], BF16)
    Wo = S.sbuf("Wo", [128, 4, D], BF16)
    for kt in range(8):
        cast_load(S, "pool", "w", Wq[:, kt, :], wq[kt * 128:(kt + 1) * 128, :], ("Wq", kt))
        cast_load(S, "pool", "w", Wk[:, kt, :], wk[kt * 128:(kt + 1) * 128, :], ("Wk", kt))
        cast_load(S, "pool", "w", Wv[:, kt, :], wv[kt * 128:(kt + 1) * 128, :], ("Wv", kt))
    for ft in range(4):
        cast_load(S, "pool", "w", Wo[:, ft, :], wo[ft * 128:(ft + 1) * 128, :], ("Wo", ft))
    lam = S.sbuf("lam_sb", [128, 256], F32)
    lsc = S.sbuf("lsc", [128, 8], F32)
    gsc = S.sbuf("gsc", [128, 128], F32)
    tfar = S.sbuf("tfar_sb", [128, 16], F32)
    S.dma("sp", "c0", lambda e: e.dma_start(out=lam[:], in_=lam_d), writes=["lam"])
    S.dma("sp", "c0", lambda e: e.dma_start(out=gsc[:], in_=sg_d), writes=["gsc"])
    S.dma("sp", "c0", lambda e: e.dma_start(out=tfar[:], in_=tfar_d), writes=["tfar"])
    S.op("dve", lambda e: e.tensor_tensor(out=lam[:, 0:64], in0=lam[:, 0:64], in1=lam[:, 64:128], op=ALU.mult),
         reads=["lam"], writes=["lam"])
    S.op("dve", lambda e: e.tensor_tensor(out=lam[:, 128:192], in0=lam[:, 128:192], in1=lam[:, 192:256], op=ALU.mult),
         reads=["lam"], writes=["lam"])
    S.op("dve", lambda e: e.reduce_sum(out=lsc[:, 0:1], in_=lam[:, 0:64], axis=AX.X), reads=["lam"], writes=["lsc"])
    S.op("dve", lambda e: e.reduce_sum(out=lsc[:, 1:2], in_=lam[:, 128:192], axis=AX.X), reads=["lam", "lsc"], writes=["lsc"])
    S.op("act", lambda e: e.activation(out=lsc[:, 2:4], in_=lsc[:, 0:2], func=AF.Exp), reads=["lsc"], writes=["lsc"])
    S.op("dve", lambda e: e.tensor_tensor(out=lsc[:, 4:5], in0=lsc[:, 3:4], in1=lsc[:, 2:3], op=ALU.subtract),
         reads=["lsc"], writes=["lsc"])
    S.op("dve", lambda e: e.tensor_scalar(out=lsc[:, 4:5], in0=lsc[:, 4:5], scalar1=-lam_init, scalar2=None, op0=ALU.add),
         reads=["lsc"], writes=["lsc"])
    S.op("dve", lambda e: e.tensor_scalar(out=gsc[:], in0=gsc[:], scalar1=1.0 - lam_init, scalar2=None, op0=ALU.mult),
         reads=["gsc"], writes=["gsc"])

    hT = [S.sbuf(f"hT{i}", [128, 8, 512], BF16) for i in range(2)]
    qst = [S.sbuf(f"qst{i}", [128, 512], BF16) for i in range(2)]
    vst = [S.sbuf(f"vst{i}", [128, 4, 130], BF16) for i in range(2)]
    for i in range(2):
        S.op("pool", lambda e, i=i: e.memset(vst[i][:, :, 128:130], 1.0), writes=[("vst1", i)])
    nq = 0
    nv = 0
    for c in range(S_LEN // 512):
        hs = c % 2
        for tt in range(4):
            P.tile(c * 4 + tt, lambda k0, n, tt=tt, hs=hs: hT[hs][:, k0:k0 + n, tt * 128:(tt + 1) * 128], ("hT", hs))
        for h in range(4):
            for which, Wm, dst, scale in (("Wq", Wq, QTd, 0.125), ("Wk", Wk, KTd, 1.0)):
                b = nq % 2
                nq += 1

                def f_mm(e, Wm=Wm, h=h, b=b, hs=hs):
                    ins = None
                    for kt in range(8):
                        ins = e.matmul(out=bk[b][:], lhsT=Wm[:, kt, h * 128:(h + 1) * 128], rhs=hT[hs][:, kt, :],
                                       start=(kt == 0), stop=(kt == 7))
                    return ins
                S.op("pe", f_mm, reads=[("hT", hs)] + [(which, kt) for kt in range(8)], writes=[BK(b)])
                S.op("act", lambda e, b=b, scale=scale: e.activation(out=qst[b][:], in_=bk[b][:], func=AF.Copy, scale=scale),
                     reads=[BK(b)], writes=[("qst", b)])
                S.dma("sp", "stq", lambda e, dst=dst, h=h, c=c, b=b: e.dma_start(out=dst[h, :, c * 512:(c + 1) * 512], in_=qst[b][:]),
                      reads=[("qst", b)], pwrites=[(which + "d", h)])
        for tt in range(4):
            t = c * 4 + tt
            b = 2 + nv % 2
            vs = nv % 2
            nv += 1

            def f_mmv(e, tt=tt, b=b, hs=hs):
                ins = None
                for kt in range(8):
                    ins = e.matmul(out=bk[b][:], lhsT=hT[hs][:, kt, tt * 128:(tt + 1) * 128], rhs=Wv[:, kt, :],
                                   start=(kt == 0), stop=(kt == 7))
                return ins
            S.op("pe", f_mmv, reads=[("hT", hs)] + [("Wv", kt) for kt in range(8)], writes=[BK(b)])
            S.op("act", lambda e, b=b, vs=vs: e.activation(out=vst[vs][:, :, 0:128],
                                                           in_=bk[b][:].rearrange("p (h v) -> p h v", h=4), func=AF.Copy),
                 reads=[BK(b), ("vst1", vs)], writes=[("vst", vs)])
            S.dma("sp", "stv", lambda e, t=t, vs=vs: e.dma_start(
                out=Vd[:, :, t * 130:(t + 1) * 130].rearrange("h p n -> p h n"), in_=vst[vs][:]),
                reads=[("vst", vs), ("vst1", vs)], pwrites=["Vd"])

    QT = [S.sbuf(f"QT{i}", [128, S_LEN], BF16) for i in range(2)]
    KT = [S.sbuf(f"KT{i}", [128, S_LEN], BF16) for i in range(2)]
    VA = [S.sbuf(f"VA{i}", [128, NT, 130], BF16) for i in range(2)]
    strip = [S.sbuf(f"strip{i}", [128, DIFF_STRIP], F32) for i in range(2)]
    PT = [S.sbuf(f"PT{i}", [128, 512], BF16) for i in range(3)]
    tmp = [S.sbuf(f"tmpb{i}", [128, 512], F32) for i in range(2)]
    o0 = S.sbuf("o0", [128, 4, 128], F32)
    on = [S.sbuf(f"on{i}", [128, 4, 128], F32) for i in range(2)]
    junk = S.sbuf("junk", [128, 128], F32)
    sc = S.sbuf("sc", [128, 16], F32)

    tiles = [(h, c, i, j) for h in range(4) for c in range(8) for i in range(2) for j in range(NT)]

    def emit_S(n):
        h, c, i, j = tiles[n]
        hs = h % 2
        b = n % 2
        rs = slice(64 * i, 64 * i + 64)
        S.op("pe", lambda e: e.matmul(out=bk[b][:], lhsT=KT[hs][rs, j * 128:(j + 1) * 128],
                                      rhs=QT[hs][rs, c * 512:(c + 1) * 512], start=True, stop=True),
             reads=[("QT", hs), ("KT", hs)], writes=[BK(b)])

    for h in range(4):
        hs = h % 2
        S.dma("sp", "ldq", lambda e, h=h, hs=hs: e.dma_start(out=QT[hs][:], in_=QTd[h]), reads=[("Wqd", h)], writes=[("QT", hs)])
        S.dma("sp", "ldk", lambda e, h=h, hs=hs: e.dma_start(out=KT[hs][:], in_=KTd[h]), reads=[("Wkd", h)], writes=[("KT", hs)])
        S.dma("sp", "ldv", lambda e, h=h, hs=hs: e.dma_start(out=VA[hs][:].rearrange("p t n -> p (t n)"), in_=Vd[h]),
              reads=["Vd"], writes=[("VA", hs)])
        for i in range(2):
            S.dma("sp", "lds", lambda e, h=h, i=i: e.dma_start(out=strip[i][:], in_=strips_d[h * 2 + i]), writes=[("strip", i)])
        base = h * 8 * 2 * NT
        emit_S(base)
        for c in range(8):
            for i in range(2):
                for j in range(NT):
                    n = base + (c * 2 + i) * NT + j
                    if n + 1 < len(tiles) and tiles[n + 1][0] == h:
                        emit_S(n + 1)
                    b = n % 2
                    pt = n % 3
                    dj = j - 4 * c
                    if -5 <= dj <= 8:
                        off = 1024 - 128 * dj
                        tb = n % 2
                        S.op("dve", lambda e, b=b, tb=tb, off=off, i=i: e.tensor_tensor(
                            out=tmp[tb][:], in0=bk[b][:], in1=strip[i][:, off:off + 512], op=ALU.add),
                            reads=[BK(b), ("strip", i)], writes=[("tmp", tb)])
                        S.op("act", lambda e, tb=tb, pt=pt: e.activation(out=PT[pt][:], in_=tmp[tb][:], func=AF.Exp),
                             reads=[("tmp", tb)], writes=[("PT", pt)])
                    else:
                        col = (h * 2 + i) * 2 + (1 if dj > 0 else 0)
                        S.op("act", lambda e, b=b, pt=pt, col=col: e.activation(out=PT[pt][:], in_=bk[b][:], func=AF.Exp,
                                                                                bias=tfar[:, col:col + 1]),
                             reads=[BK(b), "tfar"], writes=[("PT", pt)])

                    def f_av(e, pt=pt, j=j, hs=hs):
                        ins = None
                        for qb in range(4):
                            ins = e.matmul(out=bk[2 + qb][:, 0:129], lhsT=PT[pt][:, qb * 128:(qb + 1) * 128],
                                           rhs=VA[hs][:, j, 0:129], start=(j == 0), stop=(j == NT - 1))
                        return ins
                    S.op("pe", f_av, reads=[("PT", pt), ("VA", hs)], pwrites=[BK(2), BK(3), BK(4), BK(5)])
                os_ = c % 2
                for qb in range(4):
                    acc = bk[2 + qb]
                    if i == 0:
                        S.op("dve", lambda e, acc=acc, qb=qb: e.reciprocal(out=sc[:, qb:qb + 1], in_=acc[:, 128:129]),
                             reads=[BK(2 + qb)], writes=[("sc", qb)])
                        S.op("dve", lambda e, acc=acc, qb=qb: e.tensor_scalar(
                            out=o0[:, qb, :], in0=acc[:, 0:128], scalar1=sc[:, qb:qb + 1], scalar2=None, op0=ALU.mult),
                            reads=[BK(2 + qb), ("sc", qb)], writes=[("o0", qb)])
                    else:
                        S.op("dve", lambda e, acc=acc, qb=qb: e.reciprocal(out=sc[:, qb:qb + 1], in_=acc[:, 128:129]),
                             reads=[BK(2 + qb)], writes=[("sc", qb)])
                        S.op("dve", lambda e, qb=qb: e.tensor_scalar(
                            out=sc[:, 4 + qb:5 + qb], in0=sc[:, qb:qb + 1], scalar1=lsc[:, 4:5], scalar2=None, op0=ALU.mult),
                            reads=[("sc", qb), "lsc"], writes=[("sc2", qb)])
                        S.op("dve", lambda e, acc=acc, qb=qb: e.scalar_tensor_tensor(
                            out=o0[:, qb, :], in0=acc[:, 0:128], scalar=sc[:, 4 + qb:5 + qb], in1=o0[:, qb, :],
                            op0=ALU.mult, op1=ALU.add),
                            reads=[BK(2 + qb), ("sc2", qb), ("o0", qb)], writes=[("o0", qb)])
                        S.op("act", lambda e, qb=qb: e.activation(out=junk[:], in_=o0[:, qb, :], func=AF.Square,
                                                                  accum_out=sc[:, 8 + qb:9 + qb]),
                             reads=[("o0", qb)], writes=["junk", ("sc3", qb)])
                        S.op("dve", lambda e, qb=qb: e.tensor_scalar(
                            out=sc[:, 8 + qb:9 + qb], in0=sc[:, 8 + qb:9 + qb], scalar1=1.0 / 128.0, scalar2=EPS,
                            op0=ALU.mult, op1=ALU.add), reads=[("sc3", qb)], writes=[("sc3", qb)])
                        S.op("act", lambda e, qb=qb: e.activation(out=sc[:, 8 + qb:9 + qb], in_=sc[:, 8 + qb:9 + qb], func=AF.Sqrt),
                             reads=[("sc3", qb)], writes=[("sc3", qb)])
                        S.op("dve", lambda e, qb=qb: e.reciprocal(out=sc[:, 12 + qb:13 + qb], in_=sc[:, 8 + qb:9 + qb]),
                             reads=[("sc3", qb)], writes=[("sc4", qb)])
                        S.op("dve", lambda e, qb=qb, os_=os_: e.scalar_tensor_tensor(
                            out=on[os_][:, qb, :], in0=o0[:, qb, :], scalar=sc[:, 12 + qb:13 + qb], in1=gsc[:],
                            op0=ALU.mult, op1=ALU.mult),
                            reads=[("o0", qb), ("sc4", qb), "gsc"], pwrites=[("on", os_)])
                if i == 1:
                    S.dma("act", "ston", lambda e, c=c, h=h, os_=os_: e.dma_start(
                        out=ONd[c * 512:(c + 1) * 512, h * 128:(h + 1) * 128].rearrange("(q p) v -> p q v", p=128),
                        in_=on[os_][:]), reads=[("on", os_)], pwrites=["ONd"])
    OTt = [S.sbuf(f"OTt{i}", [128, 4, 128], BF16) for i in range(2)]
    for t in range(NT):
        sl = t % 2
        ont = P.pat[sl]
        pot = P.xt[sl]
        S.dma("sp", "ldon", lambda e, t=t, ont=ont: e.dma_start(out=ont[:, 0:512], in_=ONd[t * 128:(t + 1) * 128, :]),
              reads=["ONd"], writes=[("pat", sl)])

        def f_tr(e, ont=ont):
            ins = None
            for ft in range(4):
                ins = e.transpose(out=bk[6][:, ft * 128:(ft + 1) * 128], in_=ont[:, ft * 128:(ft + 1) * 128],
                                  identity=P.ident[:])
            return ins
        S.op("pe", f_tr, reads=[("pat", sl), "ident"], writes=[BK(6)])
        S.op("act", lambda e, sl=sl: e.activation(out=OTt[sl][:], in_=bk[6][:].rearrange("p (f n) -> p f n", f=4), func=AF.Copy),
             reads=[BK(6)], writes=[("OTt", sl)])
        for nh in range(2):
            def f_mo(e, nh=nh, sl=sl):
                ins = None
                for ft in range(4):
                    ins = e.matmul(out=bk[nh][:], lhsT=OTt[sl][:, ft, :], rhs=Wo[:, ft, nh * 512:(nh + 1) * 512],
                                   start=(ft == 0), stop=(ft == 3))
                return ins
            S.op("pe", f_mo, reads=[("OTt", sl)] + [("Wo", ft) for ft in range(4)], writes=[BK(nh)])
            S.op("dve", lambda e, nh=nh, pot=pot: e.tensor_copy(out=pot[:, nh * 512:(nh + 1) * 512], in_=bk[nh][:]),
                 reads=[BK(nh)], pwrites=[("xt", sl)])
        S.dma("act", "sto", lambda e, t=t, pot=pot: e.dma_start(out=po[t * 128:(t + 1) * 128, :], in_=pot[:]),
              reads=[("xt", sl)])
    S.emit()
    S.close()
    return nc, S


def rel_bucket_np(rel):
    import math
    half, max_exact = 16, 8
    rel = np.asarray(rel, dtype=np.int64)
    ret = np.where(rel > 0, half, 0)
    n = np.abs(rel)
    nf = np.maximum(n, 1).astype(np.float32)
    large = max_exact + (np.log(nf / np.float32(max_exact)) / np.float32(math.log(1024 / max_exact))
                         * np.float32(half - max_exact)).astype(np.int32)
    large = np.minimum(large, half - 1)
    return ret + np.where(n < max_exact, n, large)


IDENT = np.eye(128, dtype=np.float32)


def gT_of(g):
    return np.ascontiguousarray(np.asarray(g, np.float32).reshape(8, 128).T)


def diff_inputs(hh, qkv_w, lam, subln_g, out_w, rel_bias):
    hs = slice(512 * hh, 512 * hh + 512)
    kl = np.arange(128)[:, None]
    m = np.arange(DIFF_STRIP)[None, :]
    idx = rel_bucket_np(kl - m + 1024)
    cols = [2 * h + i for h in range(4 * hh, 4 * hh + 4) for i in range(2)]
    strips = np.ascontiguousarray(np.transpose(rel_bias[idx][:, :, cols], (2, 0, 1))).astype(np.float32)
    tfar = np.zeros((128, 16), np.float32)
    for ci, col in enumerate(cols):
        tfar[:, 2 * ci + 0] = rel_bias[15, col]
        tfar[:, 2 * ci + 1] = rel_bias[31, col]
    return {
        "wq": np.ascontiguousarray(qkv_w[:, 0 * D:1 * D][:, hs]),
        "wk": np.ascontiguousarray(qkv_w[:, 1 * D:2 * D][:, hs]),
        "wv": np.ascontiguousarray(qkv_w[:, 2 * D:3 * D][:, hs]),
        "wo": np.ascontiguousarray(out_w[hs, :]),
        "lam": np.ascontiguousarray(np.broadcast_to(lam.reshape(1, 256), (128, 256))).astype(np.float32),
        "sg": np.ascontiguousarray(np.broadcast_to(subln_g.reshape(1, 128), (128, 128))).astype(np.float32),
        "tfar": tfar,
        "strips": strips,
    }


DIL = (1, 4, 16)


def ss(start, cnt, step):
    return slice(start, start + (cnt - 1) * step + 1, step)
NEG = -30000.0


def dil_inputs(hh, g, qkv_w, rel_bias):
    W = qkv_w.reshape(D, 3, 3, 16, 64)[:, :, g, 8 * hh:8 * hh + 8, :]
    wd = np.ascontiguousarray(W.reshape(D, 3 * 512))
    kl = np.arange(128)[:, None]
    ql = np.arange(256)[None, :]
    delta = 64 + kl - ql
    valid = np.abs(delta) <= 64
    masks = np.zeros((4, 128, 512), np.float32)
    medge = np.zeros((4, 64, 256), np.float32)
    idx = rel_bucket_np(delta * DIL[g])
    for hl in range(8):
        m = np.where(valid, rel_bias[idx, 8 * hh + hl], np.float32(NEG)).astype(np.float32)
        hp, ab = hl // 2, hl % 2
        masks[hp, :, ab * 256:(ab + 1) * 256] = m
        medge[hp, :, ab * 128:(ab + 1) * 128] = m[64:128, 128:256]
    return {"wd": wd, "masks": masks, "medge": medge}


def build_dilc():
    nc = bass.Bass("TRN2", target_bir_lowering=False)
    S = Sched(nc)
    bk = [S.psum(f"bk{i}", [128, 512], F32) for i in range(2)]
    BK = lambda i: ("bk", i)
    un = [nc.dram_tensor(f"un{g}", [4, 128, S_LEN], F32, kind="ExternalInput").ap() for g in range(3)]
    ud = [nc.dram_tensor(f"ud{g}", [4, 128, S_LEN], F32, kind="ExternalInput").ap() for g in range(3)]
    wo = nc.dram_tensor("wo", [512, D], F32, kind="ExternalInput").ap()
    po = nc.dram_tensor("po", [S_LEN, D], F32, kind="ExternalOutput").ap()
    Wo = S.sbuf("Wo", [128, 4, D], BF16)
    for ft in range(4):
        cast_load(S, "pool", "w", Wo[:, ft, :], wo[ft * 128:(ft + 1) * 128, :], ("Wo", ft))
    OT = S.sbuf("OT", [128, 4, S_LEN], BF16)
    nt = [[S.sbuf(f"nt{g}_{i}", [128, 1024], F32) for i in range(2)] for g in range(3)]
    dt_ = [[S.sbuf(f"dt{g}_{i}", [128, 1024], F32) for i in range(2)] for g in range(3)]
    pot = [S.sbuf(f"pot{i}", [128, D], F32) for i in range(2)]
    k = 0
    for hp in range(4):
        for c4 in range(4):
            sl = k % 2
            k += 1
            cs = slice(c4 * 1024, (c4 + 1) * 1024)
            for g in range(3):
                S.dma("sp", f"ln{g}", lambda e, g=g, hp=hp, cs=cs, sl=sl: e.dma_start(out=nt[g][sl][:], in_=un[g][hp, :, cs]),
                      writes=[("nt", g, sl)])
                S.dma("sp", f"ld{g}", lambda e, g=g, hp=hp, cs=cs, sl=sl: e.dma_start(out=dt_[g][sl][:], in_=ud[g][hp, :, cs]),
                      writes=[("dt", g, sl)])
            for g in (1, 2):
                S.op("pool", lambda e, g=g, sl=sl: e.tensor_tensor(out=nt[0][sl][:], in0=nt[0][sl][:], in1=nt[g][sl][:], op=ALU.add),
                     reads=[("nt", 0, sl), ("nt", g, sl)], writes=[("nt", 0, sl)])
                S.op("dve", lambda e, g=g, sl=sl: e.tensor_tensor(out=dt_[0][sl][:], in0=dt_[0][sl][:], in1=dt_[g][sl][:], op=ALU.add),
                     reads=[("dt", 0, sl), ("dt", g, sl)], writes=[("dt", 0, sl)])
            S.op("dve", lambda e, sl=sl: e.reciprocal(out=dt_[0][sl][:], in_=dt_[0][sl][:]), reads=[("dt", 0, sl)], writes=[("dt", 0, sl)])
            S.op("dve", lambda e, sl=sl, hp=hp, cs=cs: e.tensor_tensor(out=OT[:, hp, cs], in0=nt[0][sl][:], in1=dt_[0][sl][:], op=ALU.mult),
                 reads=[("nt", 0, sl), ("dt", 0, sl)], pwrites=["OT"])
    for t in range(NT):
        sl = t % 2
        for nh in range(2):
            def f_mo(e, nh=nh, t=t):
                ins = None
                for ft in range(4):
                    ins = e.matmul(out=bk[nh][:], lhsT=OT[:, ft, t * 128:(t + 1) * 128], rhs=Wo[:, ft, nh * 512:(nh + 1) * 512],
                                   start=(ft == 0), stop=(ft == 3))
                return ins
            S.op("pe", f_mo, reads=["OT"] + [("Wo", ft) for ft in range(4)], writes=[BK(nh)])
            S.op("act", lambda e, nh=nh, sl=sl: e.activation(out=pot[sl][:, nh * 512:(nh + 1) * 512], in_=bk[nh][:], func=AF.Copy),
                 reads=[BK(nh)], pwrites=[("pot", sl)])
        S.dma("act", "sto", lambda e, t=t, sl=sl: e.dma_start(out=po[t * 128:(t + 1) * 128, :], in_=pot[sl][:]),
              reads=[("pot", sl)])
    S.emit()
    S.close()
    return nc, S


def build_dil(mode="add", grp=0, dbg=None):
    dbg = dbg or {}
    dbg = dict(dbg)
    dbg["groups"] = (grp,)
    nc = bass.Bass("TRN2", target_bir_lowering=False)
    S = Sched(nc)
    pr = [S.psum(f"pr{i}", [128, 1024], F32) for i in range(2)]
    bk = [pr[0][:, 0:512], pr[0][:, 512:1024], pr[1][:, 0:512], pr[1][:, 512:1024]]
    bk += [S.psum(f"bk{i}", [128, 512], F32)[:] for i in range(4, 8)]
    BK = lambda i: ("bk", i)
    P = Prologue(S, nc, mode, bk[6:8], [BK(6), BK(7)])
    wd = nc.dram_tensor("wd", [D, 3 * 512], F32, kind="ExternalInput").ap()
    un_d = nc.dram_tensor("un", [4, 128, S_LEN], F32, kind="ExternalOutput").ap()
    ud_d = nc.dram_tensor("ud", [4, 128, S_LEN], F32, kind="ExternalOutput").ap()
    masks_d = nc.dram_tensor("masks", [4, 128, 512], F32, kind="ExternalInput").ap()
    medge_d = nc.dram_tensor("medge", [4, 64, 256], F32, kind="ExternalInput").ap()
    XTd = nc.dram_tensor("XTd", [3, 4, 128, S_LEN], BF16).ap()

    hTf = S.sbuf("hTf", [128, 8, S_LEN], BF16)
    ones = S.sbuf("ones", [128, 64], BF16)
    S.op("pool", lambda e: e.memset(ones[:], 1.0), writes=["ones"])
    identb = S.sbuf("identb", [128, 128], BF16)
    S.op("dve", lambda e: e.tensor_copy(out=identb[:], in_=P.ident[:]), reads=["ident"], writes=["identb"])
    for t in range(NT):
        P.tile(t, lambda k0, n, t=t: hTf[:, k0:k0 + n, t * 128:(t + 1) * 128], ("hTf", 0), )
    Ws = [S.sbuf(f"Ws{i}", [128, 8, 512], BF16) for i in range(2)]
    qst = [S.sbuf(f"qst{i}", [128, 512], BF16) for i in range(2)]
    nq = 0
    for tg in range(3):
        ws = tg % 2
        for kt in range(8):
            cast_load(S, "pool", "w", Ws[ws][:, kt, :], wd[kt * 128:(kt + 1) * 128, tg * 512:(tg + 1) * 512], ("Ws", ws, kt))
        scale = 0.125 if tg == 0 else 1.0
        for hp in range(4):
            for c in range(8):
                b = nq % 2
                nq += 1

                def f_mm(e, ws=ws, hp=hp, b=b, c=c):
                    ins = None
                    for kt in range(8):
                        ins = e.matmul(out=bk[b][:], lhsT=Ws[ws][:, kt, hp * 128:(hp + 1) * 128],
                                       rhs=hTf[:, kt, c * 512:(c + 1) * 512], start=(kt == 0), stop=(kt == 7))
                    return ins
                S.op("pe", f_mm, reads=[("hTf", 0)] + [("Ws", ws, kt) for kt in range(8)], writes=[BK(b)])
                S.op("act", lambda e, b=b, scale=scale: e.activation(out=qst[b][:], in_=bk[b][:], func=AF.Copy, scale=scale),
                     reads=[BK(b)], writes=[("qst", b)])
                S.dma("sp", "stq", lambda e, tg=tg, hp=hp, c=c, b=b: e.dma_start(
                    out=XTd[tg, hp, :, c * 512:(c + 1) * 512], in_=qst[b][:]),
                    reads=[("qst", b)], pwrites=[("XTd", tg, hp)])

    def sub(i):
        return hTf[:, i, :]
    QT = [sub(0), sub(1)]
    KT = [sub(2), sub(3)]
    VT = [sub(4), sub(5)]
    Unum = S.sbuf("Unum", [128, S_LEN], F32)
    Uden = S.sbuf("Uden", [128, S_LEN], F32)
    Vg = [S.sbuf(f"Vg{i}", [128, 48, 128], BF16) for i in range(2)]
    M2 = [S.sbuf(f"M2{i}", [128, 512], F32) for i in range(2)]
    Me = [S.sbuf(f"Me{i}", [64, 256], F32) for i in range(2)]
    PT = [S.sbuf(f"PT{i}", [128, 512], BF16) for i in range(3)]
    tmp = [S.sbuf(f"tmpb{i}", [128, 512], F32) for i in range(2)]
    ntile = 0
    nav = 0
    nu = 0
    for hp in range(dbg.get("nhp", 4)):
        for g, d in enumerate(DIL):
            if g not in dbg.get("groups", (0, 1, 2)):
                continue
            u = nu % 2
            nu += 1
            n = S_LEN // d
            nb = n // 128
            for nm, tiles_, tgi in (("QT", QT, 0), ("KT", KT, 1), ("VT", VT, 2)):
                S.dma("sp", "ld" + nm, lambda e, dst=tiles_[u], tgi=tgi, hp=hp: e.dma_start(out=dst, in_=XTd[tgi, hp]),
                      reads=[("XTd", tgi, hp)], writes=[(nm, u)], pwrites=[("hTf", 0)])
            S.dma("sp", "ldm", lambda e, u=u, g=g, hp=hp: e.dma_start(out=M2[u][:], in_=masks_d[hp]), writes=[("M2", u)])
            S.dma("sp", "ldm", lambda e, u=u, g=g, hp=hp: e.dma_start(out=Me[u][:], in_=medge_d[hp]), writes=[("Me", u)])
            vtb = bk[4].bitcast(BF16)
            specs = []
            for r in range(d):
                for a in range(nb + 1):
                    if a == 0:
                        specs.append((r * (nb + 1) + a, r, 64))
                    elif a == nb:
                        specs.append((r * (nb + 1) + a, (n - 64) * d + r, 64))
                    else:
                        specs.append((r * (nb + 1) + a, (128 * a - 64) * d + r, 128))
            groups_ = []
            cur_ = []
            for sp_ in specs:
                if sp_[2] == 64:
                    if cur_:
                        groups_.append(cur_)
                        cur_ = []
                    groups_.append([sp_])
                else:
                    cur_.append(sp_)
                    if len(cur_) == 4:
                        groups_.append(cur_)
                        cur_ = []
            if cur_:
                groups_.append(cur_)
            for grp in (groups_ if dbg.get("vt", True) else []):
                rows_ = grp[0][2]

                def f_tr(e, grp=grp, u=u, d=d):
                    ins = None
                    for k, (ti, start, cnt) in enumerate(grp):
                        ins = e.transpose(out=vtb[0:cnt, k * 128:(k + 1) * 128],
                                          in_=VT[u][:, ss(start, cnt, d)], identity=identb[:])
                    return ins
                S.op("pe", f_tr, reads=[("VT", u), "identb"], writes=[BK(4)])
                S.op("act", lambda e, grp=grp, u=u, rows_=rows_: e.activation(
                    out=Vg[u][0:rows_, grp[0][0]:grp[0][0] + len(grp), :],
                    in_=vtb[0:rows_, 0:len(grp) * 128].rearrange("p (k f) -> p k f", k=len(grp)), func=AF.Copy),
                    reads=[BK(4)], pwrites=[("Vg", u)])
            qrows = (slice(0, 64), slice(64, 128))
            for r in range(d if dbg.get("sc", True) else 0):
                prev = None
                for a in range(nb + 1):
                    b = ntile % 2
                    pt = ntile % 3
                    tb = ntile % 2
                    ntile += 1
                    edge = (a == 0 or a == nb)
                    if a == 0:
                        kstart, kcnt, qstart, qcnt = r, 64, r, 128
                    elif a == nb:
                        kstart, kcnt, qstart, qcnt = (n - 64) * d + r, 64, (n - 128) * d + r, 128
                    else:
                        kstart, kcnt, qstart, qcnt = (128 * a - 64) * d + r, 128, (128 * a - 128) * d + r, 256

                    def f_s(e, b=b, u=u, d=d, kstart=kstart, kcnt=kcnt, qstart=qstart, qcnt=qcnt):
                        ins = None
                        for ab in range(2):
                            ins = e.matmul(out=pr[b][0:kcnt, ab * 512:ab * 512 + qcnt],
                                           lhsT=KT[u][qrows[ab], ss(kstart, kcnt, d)],
                                           rhs=QT[u][qrows[ab], ss(qstart, qcnt, d)], start=True, stop=True)
                        return ins
                    if edge and dbg.get("noedge"):
                        prev = (pt, a)
                        continue
                    S.op("pe", f_s, reads=[("QT", u), ("KT", u)], writes=[BK(2 * b), BK(2 * b + 1)])
                    if dbg.get("nodve"):
                        prev = (pt, a)
                        continue
                    pk = slice(0, kcnt)
                    hq = qcnt
                    pin = pr[b][pk, :].rearrange("p (h q) -> p h q", h=2)[:, :, 0:hq]
                    if a == 0:
                        msk = Me[u][:, :].rearrange("p (h q) -> p h q", h=2)
                        mkey = ("Me", u)
                    else:
                        msk = M2[u][pk, :].rearrange("p (h q) -> p h q", h=2)[:, :, 0:hq]
                        mkey = ("M2", u)
                    tv = tmp[tb][pk, 0:2 * hq]
                    pv = PT[pt][pk, 0:2 * hq]
                    tout = tv.rearrange("p (h q) -> p h q", h=2)
                    S.op("dve", lambda e, pin=pin, tout=tout, msk=msk: e.tensor_tensor(out=tout, in0=pin, in1=msk, op=ALU.add),
                         reads=[BK(2 * b), BK(2 * b + 1), mkey], writes=[("tmp", tb)])
                    S.op("act", lambda e, tv=tv, pv=pv: e.activation(out=pv, in_=tv, func=AF.Exp),
                         reads=[("tmp", tb)], writes=[("PT", pt)])
                    cur = (pt, a)
                    if a >= 1 and dbg.get("av", True):
                        qb = a - 1
                        slot = nav % 2
                        bank = 5 + (nav // 2) % 2
                        nav += 1
                        ppt, pa = prev
                        ti0 = r * (nb + 1) + pa
                        ti1 = r * (nb + 1) + a

                        def f_av(e, ppt=ppt, pa=pa, pt=pt, a=a, nb=nb, ti0=ti0, ti1=ti1, slot=slot, bank=bank, u=u):
                            ins = None
                            cs = slice(slot * 128, (slot + 1) * 128)
                            ds_ = slice(256 + slot * 128, 256 + (slot + 1) * 128)
                            for ab in range(2):
                                osl = slice(64 * ab, 64 * ab + 64)
                                if pa == 0:
                                    k0, r0 = 64, PT[ppt][0:64, ab * 128:(ab + 1) * 128]
                                else:
                                    k0, r0 = 128, PT[ppt][:, ab * 256 + 128:ab * 256 + 256]
                                if a == nb:
                                    k1, r1 = 64, PT[pt][0:64, ab * 128:(ab + 1) * 128]
                                else:
                                    k1, r1 = 128, PT[pt][:, ab * 256:ab * 256 + 128]
                                e.matmul(out=bk[bank][osl, cs], lhsT=Vg[u][0:k0, ti0, ab * 64:(ab + 1) * 64], rhs=r0,
                                         start=True, stop=False)
                                e.matmul(out=bk[bank][osl, cs], lhsT=Vg[u][0:k1, ti1, ab * 64:(ab + 1) * 64], rhs=r1,
                                         start=False, stop=True)
                                e.matmul(out=bk[bank][osl, ds_], lhsT=ones[0:k0, :], rhs=r0, start=True, stop=False)
                                ins = e.matmul(out=bk[bank][osl, ds_], lhsT=ones[0:k1, :], rhs=r1, start=False, stop=True)
                            return ins
                        S.op("pe", f_av, reads=[("PT", ppt), ("PT", pt), ("Vg", u), "ones"], pwrites=[BK(bank)])
                        last_in_group = (slot == 1) or (a == nb)
                        if last_in_group and dbg.get("noevac"):
                            if slot != 1:
                                nav += 1 - slot
                        elif last_in_group:
                            nq_ = slot + 1
                            qb0 = qb - slot
                            tstart = (128 * qb0) * d + r
                            cnt = 128 * nq_
                            dn = Unum[:, ss(tstart, cnt, d)]
                            dd = Uden[:, ss(tstart, cnt, d)]
                            if True:
                                S.op("dve", lambda e, dn=dn, bank=bank, cnt=cnt: e.tensor_copy(out=dn, in_=bk[bank][:, 0:cnt]),
                                     reads=[BK(bank)], pwrites=["Unum"])
                                if not dbg.get("noact"):
                                    S.op("dve", lambda e, dd=dd, bank=bank, cnt=cnt: e.tensor_copy(out=dd, in_=bk[bank][:, 256:256 + cnt]),
                                         reads=[BK(bank)], pwrites=["Uden"])
                            else:
                                S.op("dve", lambda e, dn=dn, bank=bank, cnt=cnt: e.tensor_tensor(out=dn, in0=bk[bank][:, 0:cnt], in1=dn, op=ALU.add),
                                     reads=[BK(bank), "Unum"], pwrites=["Unum"])
                                S.op("dve", lambda e, dd=dd, bank=bank, cnt=cnt: e.tensor_tensor(out=dd, in0=bk[bank][:, 256:256 + cnt], in1=dd, op=ALU.add),
                                     reads=[BK(bank), "Uden"], pwrites=["Uden"])
                            if slot != 1:
                                nav += 1 - slot
                    prev = cur
        S.dma("sp", "sto_n", lambda e, hp=hp: e.dma_start(out=un_d[hp], in_=Unum[:]), reads=["Unum"])
        S.dma("sp", "sto_d", lambda e, hp=hp: e.dma_start(out=ud_d[hp], in_=Uden[:]), reads=["Uden"])
    S.emit()
    S.close()
    return nc, S


def ssd_inputs_a(hh, in_w, conv_w, conv_b, dt_bias):
    xs = slice(2048 + 1024 * hh, 2048 + 1024 * hh + 1024)
    bs = slice(4096 + 512 * hh, 4096 + 512 * hh + 512)
    cs = slice(5120 + 512 * hh, 5120 + 512 * hh + 512)
    zs = slice(1024 * hh, 1024 * hh + 1024)
    df = slice(6144 + 16 * hh, 6144 + 16 * hh + 16)
    db = slice(6144 + 32 + 16 * hh, 6144 + 32 + 16 * hh + 16)
    wfm = np.ascontiguousarray(np.concatenate([in_w[:, xs], in_w[:, bs], in_w[:, cs]], axis=1))
    wtm = np.ascontiguousarray(np.concatenate([in_w[:, zs], in_w[:, df], in_w[:, db]], axis=1))
    cch = np.r_[1024 * hh:1024 * hh + 1024, 2048 + 512 * hh:2048 + 512 * hh + 512, 3072 + 512 * hh:3072 + 512 * hh + 512]
    cw = np.ascontiguousarray(conv_w[:, cch].T.reshape(16, 128, 5).transpose(1, 0, 2)).astype(np.float32)
    cb = np.ascontiguousarray(conv_b[cch].reshape(16, 128).T).astype(np.float32)
    dtb = np.concatenate([dt_bias[0, 16 * hh:16 * hh + 16], dt_bias[1, 16 * hh:16 * hh + 16]])
    dtb = np.ascontiguousarray(np.broadcast_to(dtb[None, :], (128, 32))).astype(np.float32)
    return {"wfm": wfm, "wtm": wtm, "cw": cw, "cb": cb, "dtb": dtb}


def build_ssd_a(mode):
    nc = bass.Bass("TRN2", target_bir_lowering=False)
    S = Sched(nc)
    bk = [S.psum(f"bk{i}", [128, 512], F32) for i in range(8)]
    BK = lambda i: ("bk", i)
    P = Prologue(S, nc, mode, bk[6:8], [BK(6), BK(7)])
    wfm_d = nc.dram_tensor("wfm", [D, 2048], F32, kind="ExternalInput").ap()
    wtm_d = nc.dram_tensor("wtm", [D, 1056], F32, kind="ExternalInput").ap()
    cw_d = nc.dram_tensor("cw", [128, 16, 5], F32, kind="ExternalInput").ap()
    cb_d = nc.dram_tensor("cb", [128, 16], F32, kind="ExternalInput").ap()
    dtb_d = nc.dram_tensor("dtb", [128, 32], F32, kind="ExternalInput").ap()
    XTOK = nc.dram_tensor("xtok", [S_LEN, 1024], F32, kind="ExternalOutput").ap()
    ZS = nc.dram_tensor("zs", [S_LEN, 1024], F32, kind="ExternalOutput").ap()
    DT = nc.dram_tensor("dt", [S_LEN, 32], F32, kind="ExternalOutput").ap()
    BTm = nc.dram_tensor("btm", [128, 4, S_LEN], BF16, kind="ExternalOutput").ap()
    CTm = nc.dram_tensor("ctm", [128, 4, S_LEN], BF16, kind="ExternalOutput").ap()
    BTOK = nc.dram_tensor("btok", [S_LEN, 512], BF16, kind="ExternalOutput").ap()

    Wf = S.sbuf("Wf", [128, 8, 2048], BF16)
    Wt = S.sbuf("Wt", [128, 8, 1056], BF16)
    for kt in range(8):
        cast_load(S, "pool", "w", Wf[:, kt, :], wfm_d[kt * 128:(kt + 1) * 128, :], ("Wf", kt))
        cast_load(S, "pool", "w", Wt[:, kt, :], wtm_d[kt * 128:(kt + 1) * 128, :], ("Wt", kt))
    cw = S.sbuf("cw_sb", [128, 16, 5], F32)
    cb = S.sbuf("cb_sb", [128, 16], F32)
    dtb = S.sbuf("dtb_sb", [128, 32], F32)
    S.dma("sp", "c0", lambda e: e.dma_start(out=cw[:], in_=cw_d), writes=["cw"])
    S.dma("sp", "c0", lambda e: e.dma_start(out=cb[:], in_=cb_d), writes=["cb"])
    S.dma("sp", "c0", lambda e: e.dma_start(out=dtb[:], in_=dtb_d), writes=["dtb"])
    identb = S.sbuf("identb", [128, 128], BF16)
    S.op("dve", lambda e: e.tensor_copy(out=identb[:], in_=P.ident[:]), reads=["ident"], writes=["identb"])
    Dg = S.sbuf("Dg", [128, 16, 5, 128], BF16)
    for ct in range(16):
        for j in range(5):
            S.op("pool", lambda e, ct=ct, j=j: e.tensor_scalar(out=Dg[:, ct, j, :], in0=P.ident[:], scalar1=cw[:, ct, j:j + 1],
                                                               scalar2=None, op0=ALU.mult),
                 reads=["ident", "cw"], pwrites=["Dg"])
    hT = [S.sbuf(f"hT{i}", [128, 8, 512], BF16) for i in range(2)]
    R = [S.sbuf(f"R{i}", [128, 16, 516], BF16) for i in range(3)]
    S.op("pool", lambda e: e.memset(R[0][:, :, 0:2], 0.0), pwrites=[("Rl", 0)])
    zst = [S.sbuf(f"zst{i}", [128, 1024], F32) for i in range(2)]
    dst = [S.sbuf(f"dst{i}", [128, 32], F32) for i in range(2)]
    cv = [S.sbuf(f"cv{i}", [128, 512], F32) for i in range(2)]
    cvb = [S.sbuf(f"cvb{i}", [128, 512], BF16) for i in range(2)]
    xst = S.sbuf("xst", [128, 4, 1024], F32)
    bst = S.sbuf("bst", [128, 4, 512], BF16)
    nz = 0
    nf = 0
    ncv = 0

    def conv_chunk(c):
        nonlocal ncv
        Rc = R[c % 3]
        rk = [("Rm", c % 3), ("Rl", c % 3), ("Rr", c % 3)]
        for ct in range(16):
            b = 2 + ncv % 2
            s_ = ncv % 2
            ncv += 1

            def f_cv(e, ct=ct, b=b, Rc=Rc):
                ins = None
                for j in range(5):
                    ins = e.matmul(out=bk[b][:], lhsT=Dg[:, ct, j, :], rhs=Rc[:, ct, j:j + 512], start=(j == 0), stop=(j == 4))
                return ins
            S.op("pe", f_cv, reads=rk + ["Dg"], writes=[BK(b)])
            if ct < 8:
                S.op("act", lambda e, b=b, s_=s_, ct=ct: e.activation(out=cv[s_][:], in_=bk[b][:], func=AF.Silu, bias=cb[:, ct:ct + 1]),
                     reads=[BK(b), "cb"], writes=[("cv", s_)])

                def f_tr(e, s_=s_):
                    ins = None
                    for tt in range(4):
                        ins = e.transpose(out=bk[4][:, tt * 128:(tt + 1) * 128], in_=cv[s_][:, tt * 128:(tt + 1) * 128],
                                          identity=P.ident[:])
                    return ins
                S.op("pe", f_tr, reads=[("cv", s_), "ident"], writes=[BK(4)])
                S.op("dve", lambda e, ct=ct: e.tensor_copy(out=xst[:, :, ct * 128:(ct + 1) * 128],
                                                           in_=bk[4][:].rearrange("p (t c) -> p t c", t=4)),
                     reads=[BK(4)], pwrites=["xst"])
            else:
                S.op("act", lambda e, b=b, s_=s_, ct=ct: e.activation(out=cvb[s_][:], in_=bk[b][:], func=AF.Silu, bias=cb[:, ct:ct + 1]),
                     reads=[BK(b), "cb"], writes=[("cvb", s_)])
                if ct < 12:
                    g = ct - 8
                    S.dma("sp", "stb", lambda e, g=g, s_=s_, c=c: e.dma_start(out=BTm[:, g, c * 512:(c + 1) * 512], in_=cvb[s_][:]),
                          reads=[("cvb", s_)])
                    vtb = bk[5][:].bitcast(BF16)

                    def f_trb(e, s_=s_, vtb=vtb):
                        ins = None
                        for tt in range(4):
                            ins = e.transpose(out=vtb[:, tt * 128:(tt + 1) * 128], in_=cvb[s_][:, tt * 128:(tt + 1) * 128],
                                              identity=identb[:])
                        return ins
                    S.op("pe", f_trb, reads=[("cvb", s_), "identb"], writes=[BK(5)])
                    S.op("dve", lambda e, g=g, vtb=vtb: e.tensor_copy(out=bst[:, :, g * 128:(g + 1) * 128],
                                                                      in_=vtb[:, 0:512].rearrange("p (t c) -> p t c", t=4)),
                         reads=[BK(5)], pwrites=["bst"])
                else:
                    g = ct - 12
                    S.dma("sp", "stc", lambda e, g=g, s_=s_, c=c: e.dma_start(out=CTm[:, g, c * 512:(c + 1) * 512], in_=cvb[s_][:]),
                          reads=[("cvb", s_)])
        S.dma("act", "stxk", lambda e, c=c: e.dma_start(out=XTOK[c * 512:(c + 1) * 512, :].rearrange("(t p) c -> p t c", p=128),
                                                        in_=xst[:]), reads=["xst"])
        S.dma("act", "stbk", lambda e, c=c: e.dma_start(out=BTOK[c * 512:(c + 1) * 512, :].rearrange("(t p) c -> p t c", p=128),
                                                        in_=bst[:]), reads=["bst"])

    for c in range(8):
        hs = c % 2
        Rc = R[c % 3]
        for tt in range(4):
            P.tile(c * 4 + tt, lambda k0, n, tt=tt, hs=hs: hT[hs][:, k0:k0 + n, tt * 128:(tt + 1) * 128], ("hT", hs))
        for tt in range(4):
            t = c * 4 + tt
            zsl = nz % 2
            nz += 1
            for half in range(2):
                b = half

                def f_z(e, tt=tt, half=half, b=b, hs=hs):
                    ins = None
                    for kt in range(8):
                        ins = e.matmul(out=bk[b][:], lhsT=hT[hs][:, kt, tt * 128:(tt + 1) * 128],
                                       rhs=Wt[:, kt, half * 512:(half + 1) * 512], start=(kt == 0), stop=(kt == 7))
                    return ins
                S.op("pe", f_z, reads=[("hT", hs)] + [("Wt", kt) for kt in range(8)], writes=[BK(b)])
                S.op("act", lambda e, b=b, zsl=zsl, half=half: e.activation(out=zst[zsl][:, half * 512:(half + 1) * 512],
                                                                            in_=bk[b][:], func=AF.Silu),
                     reads=[BK(b)], pwrites=[("zst", zsl)])
            S.dma("sp", "stz", lambda e, t=t, zsl=zsl: e.dma_start(out=ZS[t * 128:(t + 1) * 128, :], in_=zst[zsl][:]),
                  reads=[("zst", zsl)])

            def f_dt(e, tt=tt, hs=hs):
                ins = None
                for kt in range(8):
                    ins = e.matmul(out=bk[5][:, 0:32], lhsT=hT[hs][:, kt, tt * 128:(tt + 1) * 128],
                                   rhs=Wt[:, kt, 1024:1056], start=(kt == 0), stop=(kt == 7))
                return ins
            S.op("pe", f_dt, reads=[("hT", hs)] + [("Wt", kt) for kt in range(8)], writes=[BK(5)])
            S.op("dve", lambda e, zsl=zsl: e.tensor_tensor(out=dst[zsl][:], in0=bk[5][:, 0:32], in1=dtb[:], op=ALU.add),
                 reads=[BK(5), "dtb"], writes=[("dst", zsl)])
            S.op("act", lambda e, zsl=zsl: e.activation(out=dst[zsl][:], in_=dst[zsl][:], func=AF.Exp),
                 reads=[("dst", zsl)], writes=[("dst", zsl)])
            S.op("act", lambda e, zsl=zsl: e.activation(out=dst[zsl][:], in_=dst[zsl][:], func=AF.Ln, bias=1.0),
                 reads=[("dst", zsl)], writes=[("dst", zsl)])
            S.dma("sp", "stdt", lambda e, t=t, zsl=zsl: e.dma_start(out=DT[t * 128:(t + 1) * 128, :], in_=dst[zsl][:]),
                  reads=[("dst", zsl)])
        for ct in range(16):
            b = 2 + nf % 2
            nf += 1

            def f_f(e, ct=ct, b=b, hs=hs):
                ins = None
                for kt in range(8):
                    ins = e.matmul(out=bk[b][:], lhsT=Wf[:, kt, ct * 128:(ct + 1) * 128], rhs=hT[hs][:, kt, :],
                                   start=(kt == 0), stop=(kt == 7))
                return ins
            S.op("pe", f_f, reads=[("hT", hs)] + [("Wf", kt) for kt in range(8)], writes=[BK(b)])
            S.op("dve", lambda e, ct=ct, b=b, Rc=Rc: e.tensor_copy(out=Rc[:, ct, 2:514], in_=bk[b][:]),
                 reads=[BK(b)], pwrites=[("Rm", c % 3)])
        if c >= 1:
            Rp = R[(c - 1) % 3]
            S.op("pool", lambda e, Rp=Rp, Rc=Rc: e.tensor_copy(out=Rp[:, :, 514:516], in_=Rc[:, :, 2:4]),
                 reads=[("Rm", c % 3)], pwrites=[("Rr", (c - 1) % 3)])
            S.op("pool", lambda e, Rp=Rp, Rc=Rc: e.tensor_copy(out=Rc[:, :, 0:2], in_=Rp[:, :, 512:514]),
                 reads=[("Rm", (c - 1) % 3)], pwrites=[("Rl", c % 3)])
            conv_chunk(c - 1)
    S.op("pool", lambda e: e.memset(R[7 % 3][:, :, 514:516], 0.0), pwrites=[("Rr", 7 % 3)])
    conv_chunk(7)
    S.emit()
    S.close()
    return nc, S


TRIF = np.triu(np.ones((128, 128), np.float32))
TRIB = np.tril(np.ones((128, 128), np.float32))
ONESF = np.ones((128, 128), np.float32)


def ssd_inputs_b(hh, a_log, d_skip, norm_g, out_w):
    hs = slice(16 * hh, 16 * hh + 16)
    al = np.concatenate([a_log[0, hs], a_log[1, hs]])
    return {
        "alog": np.ascontiguousarray(np.broadcast_to(al[None, :], (128, 32))).astype(np.float32),
        "dsk": np.ascontiguousarray(np.broadcast_to(d_skip[hs][None, :], (128, 16))).astype(np.float32),
        "gn": np.ascontiguousarray(np.broadcast_to(norm_g[1024 * hh:1024 * hh + 1024][None, :], (128, 1024))).astype(np.float32),
        "wout": np.ascontiguousarray(out_w[1024 * hh:1024 * hh + 1024, :]),
        "trif": TRIF, "trib": TRIB, "onesf": ONESF, "ident": IDENT,
        "negm": np.ascontiguousarray(np.concatenate([np.tile((1.0 - TRIF) * NEG, (1, 4)), np.tile((1.0 - TRIB) * NEG, (1, 4))], axis=1)).astype(np.float32),
    }


def build_ssd_b(dbg=None):
    dbg = dbg or {}
    nc = bass.Bass("TRN2", target_bir_lowering=False)
    S = Sched(nc)
    din = lambda name, shape, dt=F32: nc.dram_tensor(name, list(shape), dt, kind="ExternalInput").ap()
    xtok_d = din("xtok", [S_LEN, 1024])
    zs_d = din("zs", [S_LEN, 1024])
    dt_d = din("dt", [S_LEN, 32])
    btm_d = din("btm", [128, 4, S_LEN], BF16)
    ctm_d = din("ctm", [128, 4, S_LEN], BF16)
    btok_d = din("btok", [S_LEN, 512], BF16)
    wout_d = din("wout", [1024, D])
    alog_d = din("alog", [128, 32])
    dsk_d = din("dsk", [128, 16])
    gn_d = din("gn", [128, 1024])
    consts_d = {k: din(k, [128, 128]) for k in ("trif", "trib", "onesf", "ident")}
    negm_d = din("negm", [128, 1024])
    po = nc.dram_tensor("po", [S_LEN, D], F32, kind="ExternalOutput").ap()
    sso_d = nc.dram_tensor("sso", [128, NT], F32, kind="ExternalOutput").ap()
    YF = nc.dram_tensor("YF", [S_LEN, 1024], F32).ap()

    bk0 = S.psum("bk0", [128, 512], F32)
    pd = [S.psum(f"pd{i}", [128, 512], F32) for i in range(2)]
    pcb = S.psum("pcb", [128, 512], F32)
    py = S.psum("py", [128, 1024], F32)
    ps = S.psum("ps", [128, 1024], F32)

    cst = {}
    for k, ap in consts_d.items():
        cst[k] = S.sbuf(k + "_sb", [128, 128], F32)
        S.dma("sp", "c0", lambda e, k=k, ap=ap: e.dma_start(out=cst[k][:], in_=ap), writes=[k])
    ident, onesf = cst["ident"], cst["onesf"]
    tri = [cst["trif"], cst["trib"]]
    trik = ["trif", "trib"]
    negf = S.sbuf("negf", [128, 1024], F32)
    negm = S.sbuf("negm_sb", [128, 1024], BF16)
    identb = S.sbuf("identb", [128, 128], BF16)
    S.dma("sp", "c0", lambda e: e.dma_start(out=negf[:], in_=negm_d), writes=["negf"])
    S.op("dve", lambda e: e.tensor_copy(out=negm[:], in_=negf[:]), reads=["negf"], writes=["negm"])
    S.op("dve", lambda e: e.tensor_copy(out=identb[:], in_=ident[:]), reads=["ident"], writes=["identb"])
    Ab = S.sbuf("Ab", [128, 32], F32)
    dsk = S.sbuf("dsk_sb", [128, 16], F32)
    gn = S.sbuf("gn_sb", [128, 1024], F32)
    S.dma("sp", "c0", lambda e: e.dma_start(out=Ab[:], in_=alog_d), writes=["Ab"])
    S.dma("sp", "c0", lambda e: e.dma_start(out=dsk[:], in_=dsk_d), writes=["dsk"])
    S.dma("sp", "c0", lambda e: e.dma_start(out=gn[:], in_=gn_d), writes=["gn"])
    S.op("act", lambda e: e.activation(out=Ab[:], in_=Ab[:], func=AF.Exp), reads=["Ab"], writes=["Ab"])
    S.op("dve", lambda e: e.tensor_scalar(out=Ab[:], in0=Ab[:], scalar1=-1.0, scalar2=None, op0=ALU.mult), reads=["Ab"], writes=["Ab"])
    Wout = S.sbuf("Wout", [128, 8, D], BF16)
    for ct in range(8):
        cast_load(S, "pool", "w", Wout[:, ct, :], wout_d[ct * 128:(ct + 1) * 128, :], ("Wout", ct))

    xk = [S.sbuf(f"xk{i}", [128, 16, 64], F32) for i in range(2)]
    btc = [S.sbuf(f"btc{i}", [128, 4, 128], BF16) for i in range(2)]
    ctc = [S.sbuf(f"ctc{i}", [128, 4, 128], BF16) for i in range(2)]
    bkk = [S.sbuf(f"bkk{i}", [128, 512], BF16) for i in range(2)]
    dtc = [S.sbuf(f"dtc{i}", [128, 32], F32) for i in range(2)]
    zsc = [S.sbuf(f"zsc{i}", [128, 1024], F32) for i in range(2)]
    yfc = [S.sbuf(f"yfc{i}", [128, 1024], F32) for i in range(2)]
    adt = S.sbuf("adt", [128, 16], F32)
    csb = S.sbuf("csb", [128, 32], F32)
    sm = S.sbuf("sm", [128, 64], F32)
    X = S.sbuf("X", [128, 16, 128], F32)
    xdt = S.sbuf("xdt", [128, 16, 64], BF16)
    xdd = S.sbuf("xdd", [128, 16, 64], BF16)
    EA = [S.sbuf(f"EA{i}", [128, 4, 128], F32) for i in range(2)]
    Lr = [S.sbuf(f"Lr{i}", [128, 4, 128], F32) for i in range(2)]
    MT = [S.sbuf(f"MT{i}", [128, 4, 128], BF16) for i in range(2)]
    CdT = [S.sbuf(f"CdT{i}", [128, 4, 128], BF16) for i in range(2)]
    CBm = [S.sbuf(f"CBm{i}", [128, 128], F32) for i in range(2)]
    state = S.sbuf("state", [128, 16, 64], F32)
    stateb = S.sbuf("stateb", [128, 16, 64], BF16)
    stmp = S.sbuf("stmp", [128, 16, 64], F32)
    ysb = [S.sbuf(f"ysb{i}", [128, 1024], F32) for i in range(2)]
    t1 = S.sbuf("t1", [128, 16, 64], F32)
    ygT = S.sbuf("ygT", [128, 8, 128], BF16)
    pot = [S.sbuf(f"pot{i}", [128, D], F32) for i in range(2)]
    junk = S.sbuf("junk", [128, 1024], F32)
    sso = S.sbuf("sso_sb", [128, NT], F32)

    k = 0
    nchunks = dbg.get("nchunks", NT)
    for dir_ in range(2):
        S.op("pool", lambda e: e.memset(state[:], 0.0), writes=["state"])
        S.op("pool", lambda e: e.memset(stateb[:], 0.0), writes=["stateb"])
        order = list(range(nchunks)) if dir_ == 0 else list(range(nchunks - 1, -1, -1))
        dsl = slice(16 * dir_, 16 * dir_ + 16)
        last = 127 if dir_ == 0 else 0
        for c in order:
            sl = k % 2
            k += 1
            rows = slice(c * 128, (c + 1) * 128)
            S.dma("sp", "lx", lambda e, sl=sl, rows=rows: e.dma_start(out=xk[sl][:].rearrange("p h d -> p (h d)"), in_=xtok_d[rows, :]), writes=[("xk", sl)])
            S.dma("sp", "lbt", lambda e, sl=sl, rows=rows: e.dma_start(out=btc[sl][:], in_=btm_d[:, :, rows]), writes=[("btc", sl)])
            S.dma("sp", "lct", lambda e, sl=sl, rows=rows: e.dma_start(out=ctc[sl][:], in_=ctm_d[:, :, rows]), writes=[("ctc", sl)])
            S.dma("sp", "lbk", lambda e, sl=sl, rows=rows: e.dma_start(out=bkk[sl][:], in_=btok_d[rows, :]), writes=[("bkk", sl)])
            S.dma("sp", "ldt", lambda e, sl=sl, rows=rows: e.dma_start(out=dtc[sl][:], in_=dt_d[rows, :]), writes=[("dtc", sl)])
            if dir_ == 1:
                S.dma("sp", "lzs", lambda e, sl=sl, rows=rows: e.dma_start(out=zsc[sl][:], in_=zs_d[rows, :]), writes=[("zsc", sl)])
                S.dma("sp", "lyf", lambda e, sl=sl, rows=rows: e.dma_start(out=yfc[sl][:], in_=YF[rows, :]), reads=["YF"], writes=[("yfc", sl)])
            S.op("dve", lambda e, sl=sl, dsl=dsl: e.tensor_tensor(out=adt[:], in0=dtc[sl][:, dsl], in1=Ab[:, dsl], op=ALU.mult),
                 reads=[("dtc", sl), "Ab"], writes=["adt"])

            def f_cs(e, dir_=dir_):
                e.matmul(out=bk0[:, 0:16], lhsT=tri[dir_][:], rhs=adt[:], start=True, stop=True)
                return e.matmul(out=bk0[:, 16:32], lhsT=onesf[:], rhs=adt[:], start=True, stop=True)
            S.op("pe", f_cs, reads=["adt", trik[dir_], "onesf"], writes=["bk0"])
            S.op("dve", lambda e: e.tensor_copy(out=csb[:], in_=bk0[:, 0:32]), reads=["bk0"], writes=["csb"])
            S.op("dve", lambda e: e.tensor_scalar(out=sm[:, 0:16], in0=csb[:, 0:16], scalar1=-1.0, scalar2=None, op0=ALU.mult),
                 reads=["csb"], writes=["ncs"])
            S.op("dve", lambda e: e.tensor_tensor(out=sm[:, 16:32], in0=csb[:, 16:32], in1=csb[:, 0:16], op=ALU.subtract),
                 reads=["csb"], writes=["dec"])
            S.op("act", lambda e: e.activation(out=sm[:, 16:32], in_=sm[:, 16:32], func=AF.Exp), reads=["dec"], writes=["dec"])
            S.op("dve", lambda e, sl=sl, dsl=dsl: e.tensor_tensor(out=sm[:, 32:48], in0=sm[:, 16:32], in1=dtc[sl][:, dsl], op=ALU.mult),
                 reads=["dec", ("dtc", sl)], writes=["w2"])
            S.op("dve", lambda e: e.tensor_tensor(out=X[:], in0=ident[:].unsqueeze(1).to_broadcast([128, 16, 128]),
                                                  in1=csb[:, 0:16].unsqueeze(2).to_broadcast([128, 16, 128]), op=ALU.mult),
                 reads=["ident", "csb"], writes=["X"])
            S.op("pool", lambda e, sl=sl, dsl=dsl: e.tensor_tensor(out=xdt[:], in0=xk[sl][:],
                                                                   in1=dtc[sl][:, dsl].unsqueeze(2).to_broadcast([128, 16, 64]), op=ALU.mult),
                 reads=[("xk", sl), ("dtc", sl)], writes=["xdt"])
            S.op("pool", lambda e, sl=sl: e.tensor_tensor(out=xdd[:], in0=xk[sl][:],
                                                          in1=sm[:, 32:48].unsqueeze(2).to_broadcast([128, 16, 64]), op=ALU.mult),
                 reads=[("xk", sl), "w2"], writes=["xdd"])
            for g in range(4):
                ge = g % 2
                pdb = pd[ge]
                S.op("pe", lambda e, g=g, pdb=pdb: e.matmul(out=pdb[:], lhsT=onesf[:], rhs=X[:, 4 * g:4 * g + 4, :].rearrange("p h l -> p (h l)"),
                                                            start=True, stop=False, skip_group_check=True),
                     reads=["X", "onesf"], writes=[("pd", ge)])
                S.op("act", lambda e, ge=ge, pdb=pdb: e.activation(out=EA[ge][:].rearrange("p h l -> p (h l)"), in_=pdb[:], func=AF.Exp),
                     reads=[("pd", ge)], writes=[("EA", ge)])
                S.op("pe", lambda e, pdb=pdb, dir_=dir_: e.matmul(out=pdb[:], lhsT=identb[:], rhs=negm[:, dir_ * 512:(dir_ + 1) * 512],
                                                                   start=False, stop=True, skip_group_check=True),
                     reads=["identb", "negm"], pwrites=[("pd", ge)])
                for hd in range(4):
                    S.op("act", lambda e, ge=ge, pdb=pdb, hd=hd, g=g: e.activation(
                        out=Lr[ge][:, hd, :], in_=pdb[:, hd * 128:(hd + 1) * 128], func=AF.Exp, bias=sm[:, 4 * g + hd:4 * g + hd + 1]),
                        reads=[("pd", ge), "ncs"], pwrites=[("Lr", ge)])
                S.op("pe", lambda e, g=g, sl=sl: e.matmul(out=pcb[:, 0:128], lhsT=btc[sl][:, g, :], rhs=ctc[sl][:, g, :], start=True, stop=True),
                     reads=[("btc", sl), ("ctc", sl)], writes=["pcb"])
                S.op("dve", lambda e, ge=ge, dir_=dir_: e.tensor_tensor(out=CBm[ge][:], in0=pcb[:, 0:128], in1=tri[dir_][:], op=ALU.mult),
                     reads=["pcb", trik[dir_]], writes=[("CBm", ge)])
                S.op("dve", lambda e, ge=ge: e.scalar_tensor_tensor(
                    out=MT[ge][:], in0=Lr[ge][:], scalar=1.0, in1=CBm[ge][:].unsqueeze(1).to_broadcast([128, 4, 128]),
                    op0=ALU.mult, op1=ALU.mult),
                    reads=[("Lr", ge), ("CBm", ge)], writes=[("MT", ge)])
                S.op("pool", lambda e, ge=ge, g=g, sl=sl: e.tensor_tensor(
                    out=CdT[ge][:], in0=EA[ge][:], in1=ctc[sl][:, g, :].unsqueeze(1).to_broadcast([128, 4, 128]), op=ALU.mult),
                    reads=[("EA", ge), ("ctc", sl)], writes=[("CdT", ge)])

                def f_y(e, g=g, ge=ge):
                    ins = None
                    for hd in range(4):
                        h_ = 4 * g + hd
                        e.matmul(out=py[:, h_ * 64:(h_ + 1) * 64], lhsT=MT[ge][:, hd, :], rhs=xdt[:, h_, :], start=True, stop=False)
                        ins = e.matmul(out=py[:, h_ * 64:(h_ + 1) * 64], lhsT=CdT[ge][:, hd, :], rhs=stateb[:, h_, :], start=False, stop=True)
                    return ins
                S.op("pe", f_y, reads=[("MT", ge), ("CdT", ge), "xdt", "stateb"], pwrites=["py"])
                S.op("pe", lambda e, g=g, sl=sl: e.matmul(out=ps[:, g * 256:(g + 1) * 256], lhsT=bkk[sl][:, g * 128:(g + 1) * 128],
                                                          rhs=xdd[:, 4 * g:4 * g + 4, :].rearrange("p h d -> p (h d)"), start=True, stop=True),
                     reads=[("bkk", sl), "xdd"], pwrites=["ps"])
                S.op("act", lambda e, ge=ge, g=g, last=last: e.activation(out=sm[:, 48 + 4 * g:52 + 4 * g], in_=EA[ge][:, :, last], func=AF.Copy),
                     reads=[("EA", ge)], pwrites=["eal"])
            S.op("pool", lambda e: e.tensor_tensor(out=stmp[:], in0=state[:], in1=sm[:, 48:64].unsqueeze(2).to_broadcast([128, 16, 64]), op=ALU.mult),
                 reads=["state", "eal"], writes=["stmp"])
            S.op("dve", lambda e: e.tensor_tensor(out=state[:].rearrange("p h d -> p (h d)"), in0=ps[:], in1=stmp[:].rearrange("p h d -> p (h d)"), op=ALU.add),
                 reads=["ps", "stmp"], writes=["state"])
            S.op("act", lambda e: e.activation(out=stateb[:], in_=state[:], func=AF.Copy), reads=["state"], writes=["stateb"])
            ys = k % 2
            if dir_ == 0:
                S.op("act", lambda e, ys=ys: e.activation(out=ysb[ys][:], in_=py[:], func=AF.Copy), reads=["py"], writes=[("ysb", ys)])
                S.dma("act", "syf", lambda e, ys=ys, rows=rows: e.dma_start(out=YF[rows, :], in_=ysb[ys][:]), reads=[("ysb", ys)], pwrites=["YF"])
            else:
                S.op("pool", lambda e, sl=sl: e.tensor_tensor(out=t1[:], in0=xk[sl][:], in1=dsk[:].unsqueeze(2).to_broadcast([128, 16, 64]), op=ALU.mult),
                     reads=[("xk", sl), "dsk"], writes=["t1"])
                t1f = t1[:].rearrange("p h d -> p (h d)")
                S.op("pool", lambda e, sl=sl, t1f=t1f: e.tensor_tensor(out=t1f, in0=t1f, in1=yfc[sl][:], op=ALU.add),
                     reads=["t1", ("yfc", sl)], writes=["t1"])
                S.op("dve", lambda e, t1f=t1f, ys=ys: e.tensor_tensor(out=ysb[ys][:], in0=py[:], in1=t1f, op=ALU.add),
                     reads=["py", "t1"], writes=[("ysb", ys)])
                S.op("pool", lambda e, ys=ys, sl=sl: e.tensor_tensor(out=ysb[ys][:], in0=ysb[ys][:], in1=zsc[sl][:], op=ALU.mult),
                     reads=[("ysb", ys), ("zsc", sl)], writes=[("ysb", ys)])
                S.op("act", lambda e, ys=ys, c=c: e.activation(out=junk[:], in_=ysb[ys][:], func=AF.Square, accum_out=sso[:, c:c + 1]),
                     reads=[("ysb", ys)], writes=["junk"], pwrites=["sso"])
                S.op("pool", lambda e, ys=ys: e.tensor_tensor(out=ysb[ys][:], in0=ysb[ys][:], in1=gn[:], op=ALU.mult),
                     reads=[("ysb", ys), "gn", "junk"], writes=[("ysb", ys)])
                for hb in range(2):
                    def f_tr(e, hb=hb, ys=ys):
                        ins = None
                        for j in range(4):
                            ct = hb * 4 + j
                            ins = e.transpose(out=pd[hb][:, j * 128:(j + 1) * 128], in_=ysb[ys][:, ct * 128:(ct + 1) * 128], identity=ident[:])
                        return ins
                    S.op("pe", f_tr, reads=[("ysb", ys), "ident"], writes=[("pd", hb)])
                    S.op("act", lambda e, hb=hb: e.activation(out=ygT[:, hb * 4:hb * 4 + 4, :], in_=pd[hb][:].rearrange("p (k n) -> p k n", k=4), func=AF.Copy),
                         reads=[("pd", hb)], pwrites=["ygT"])
                psl = c % 2
                for nh in range(2):
                    def f_o(e, nh=nh):
                        ins = None
                        for ct in range(8):
                            ins = e.matmul(out=ps[:, nh * 512:(nh + 1) * 512], lhsT=ygT[:, ct, :], rhs=Wout[:, ct, nh * 512:(nh + 1) * 512],
                                           start=(ct == 0), stop=(ct == 7))
                        return ins
                    S.op("pe", f_o, reads=["ygT"] + [("Wout", ct) for ct in range(8)], pwrites=["ps"])
                S.op("dve", lambda e, psl=psl: e.tensor_copy(out=pot[psl][:], in_=ps[:]), reads=["ps"], writes=[("pot", psl)])
                S.dma("act", "sto", lambda e, rows=rows, psl=psl: e.dma_start(out=po[rows, :], in_=pot[psl][:]), reads=[("pot", psl)])
    S.dma("sp", "ssto", lambda e: e.dma_start(out=sso_d, in_=sso[:]), reads=["sso"])
    S.emit()
    S.close()
    return nc, S


def build_final():
    nc = bass.Bass("TRN2", target_bir_lowering=False)
    S = Sched(nc)
    H = S_LEN // 2
    x = nc.dram_tensor("x", [H, D], F32, kind="ExternalInput").ap()
    pa = nc.dram_tensor("pa", [H, D], F32, kind="ExternalInput").ap()
    pb = nc.dram_tensor("pb", [H, D], F32, kind="ExternalInput").ap()
    g_d = nc.dram_tensor("gb", [128, D], F32, kind="ExternalInput").ap()
    out = nc.dram_tensor("out", [H, D], F32, kind="ExternalOutput").ap()
    gb = S.sbuf("gb_sb", [128, D], F32)
    S.dma("sp", "c0", lambda e: e.dma_start(out=gb[:], in_=g_d), writes=["gb"])
    xt = [S.sbuf(f"xt{i}", [128, D], F32) for i in range(2)]
    at = [S.sbuf(f"at{i}", [128, D], F32) for i in range(2)]
    bt = [S.sbuf(f"bt{i}", [128, D], F32) for i in range(2)]
    st = [S.sbuf(f"st{i}", [128, 4], F32) for i in range(2)]
    for t in range(H // 128):
        sl = t % 2
        rows = slice(t * 128, (t + 1) * 128)
        S.dma("sp", "lx", lambda e, sl=sl, rows=rows: e.dma_start(out=xt[sl][:], in_=x[rows, :]), writes=[("xt", sl)])
        S.dma("sp", "la", lambda e, sl=sl, rows=rows: e.dma_start(out=at[sl][:], in_=pa[rows, :]), writes=[("at", sl)])
        S.dma("sp", "lb", lambda e, sl=sl, rows=rows: e.dma_start(out=bt[sl][:], in_=pb[rows, :]), writes=[("bt", sl)])
        S.op("pool", lambda e, sl=sl: e.tensor_tensor(out=at[sl][:], in0=at[sl][:], in1=bt[sl][:], op=ALU.add),
             reads=[("at", sl), ("bt", sl)], writes=[("at", sl)])
        S.op("dve", lambda e, sl=sl: e.tensor_tensor(out=xt[sl][:], in0=xt[sl][:], in1=at[sl][:], op=ALU.add),
             reads=[("xt", sl), ("at", sl)], writes=[("xt", sl)])
        S.op("act", lambda e, sl=sl: e.activation(out=at[sl][:], in_=xt[sl][:], func=AF.Square, accum_out=st[sl][:, 0:1]),
             reads=[("xt", sl)], writes=[("at", sl), ("st", sl)])
        S.op("dve", lambda e, sl=sl: e.tensor_scalar(out=st[sl][:, 1:2], in0=st[sl][:, 0:1], scalar1=1.0 / D, scalar2=EPS,
                                                     op0=ALU.mult, op1=ALU.add), reads=[("st", sl)], writes=[("st", sl)])
        S.op("act", lambda e, sl=sl: e.activation(out=st[sl][:, 2:3], in_=st[sl][:, 1:2], func=AF.Sqrt), reads=[("st", sl)], writes=[("st", sl)])
        S.op("dve", lambda e, sl=sl: e.reciprocal(out=st[sl][:, 3:4], in_=st[sl][:, 2:3]), reads=[("st", sl)], writes=[("st", sl)])
        S.op("dve", lambda e, sl=sl: e.scalar_tensor_tensor(out=bt[sl][:], in0=xt[sl][:], scalar=st[sl][:, 3:4], in1=gb[:],
                                                            op0=ALU.mult, op1=ALU.mult),
             reads=[("xt", sl), ("st", sl), "gb"], writes=[("bt", sl)])
        S.dma("act", "sto", lambda e, sl=sl, rows=rows: e.dma_start(out=out[rows, :], in_=bt[sl][:]), reads=[("bt", sl)])
    S.emit()
    S.close()
    return nc, S


_PROGS = {}


def _prog(key, fn):
    if key not in _PROGS:
        _PROGS[key] = fn()[0]
    return _PROGS[key]


def _launch(nc, in_maps):
    res = run_bass_kernel_spmd(nc, in_maps, core_ids=list(range(8)))
    return res.results


def _c(a):
    return np.ascontiguousarray(a)


def kernel(x, rel_bias, norm_mix_g, norm_mlp_g, mlp_w1, mlp_w2, ssd_in_w, ssd_conv_w, ssd_conv_b, ssd_dt_bias,
           ssd_a_log, ssd_d, ssd_norm_g, ssd_out_w, dil_qkv_w, dil_out_w, diff_qkv_w, diff_lambda, diff_subln_g,
           diff_out_w, final_norm_g):
    f = lambda a: np.asarray(a, dtype=np.float32)
    x = f(x)
    rel_bias = f(rel_bias)
    import ml_dtypes
    xs = [_c(x[c // 2]) for c in range(8)]
    part = None
    ss = None

    def pro_inputs(c, g, mode):
        m = {"x": xs[c], "ident": IDENT, "gT": gT_of(g)}
        if mode != "first":
            m["pa"] = part[c]
            m["pb"] = part[c ^ 1]
        if mode == "ssd":
            m["ssa"] = ss[c]
            m["ssb"] = ss[c ^ 1]
        return m

    for i in range(4):
        kind, j = i % 3, i // 3
        mode = "first" if i == 0 else "add"
        if kind == 0:
            nc = _prog(("ssd_a", mode), lambda: build_ssd_a(mode))
            ims = []
            for c in range(8):
                m = pro_inputs(c, f(norm_mix_g[i]), mode)
                m.update(ssd_inputs_a(c % 2, f(ssd_in_w[j]), f(ssd_conv_w[j]), f(ssd_conv_b[j]), f(ssd_dt_bias[j])))
                ims.append(m)
            ra = _launch(nc, ims)
            if mode != "first":
                xs = [_c(ra[c]["xo"]) for c in range(8)]
            nc = _prog(("ssd_b",), build_ssd_b)
            ims = []
            for c in range(8):
                m = {k: ra[c][k] for k in ("xtok", "zs", "dt", "btm", "ctm", "btok")}
                m.update(ssd_inputs_b(c % 2, f(ssd_a_log[j]), f(ssd_d[j]), f(ssd_norm_g[j]), f(ssd_out_w[j])))
                ims.append(m)
            rb = _launch(nc, ims)
            part = [_c(rb[c]["po"]) for c in range(8)]
            ss = [_c(rb[c]["sso"]) for c in range(8)]
            mlp_mode = "ssd"
        elif kind == 1:
            uns, uds = [], []
            for g in range(3):
                nc = _prog(("dil", g, mode), lambda: build_dil(mode=mode, grp=g))
                ims = []
                for c in range(8):
                    m = pro_inputs(c, f(norm_mix_g[i]), mode)
                    m.update(dil_inputs(c % 2, g, f(dil_qkv_w[j]), rel_bias))
                    ims.append(m)
                r = _launch(nc, ims)
                uns.append([r[c]["un"] for c in range(8)])
                uds.append([r[c]["ud"] for c in range(8)])
                if g == 0 and mode != "first":
                    xo_new = [_c(r[c]["xo"]) for c in range(8)]
            xs = xo_new
            nc = _prog(("dilc",), build_dilc)
            ims = []
            for c in range(8):
                hh = c % 2
                m = {"wo": _c(f(dil_out_w[j])[512 * hh:512 * hh + 512])}
                for g in range(3):
                    m[f"un{g}"] = uns[g][c]
                    m[f"ud{g}"] = uds[g][c]
                ims.append(m)
            r = _launch(nc, ims)
            part = [_c(r[c]["po"]) for c in range(8)]
            mlp_mode = "add"
        else:
            nc = _prog(("diff", i, mode), lambda: build_diff(i, mode=mode))
            ims = []
            for c in range(8):
                m = pro_inputs(c, f(norm_mix_g[i]), mode)
                m.update(diff_inputs(c % 2, f(diff_qkv_w[j]), f(diff_lambda[j]), f(diff_subln_g[j]), f(diff_out_w[j]), rel_bias))
                ims.append(m)
            r = _launch(nc, ims)
            xs = [_c(r[c]["xo"]) for c in range(8)]
            part = [_c(r[c]["po"]) for c in range(8)]
            mlp_mode = "add"
        nc = _prog(("mlp", mlp_mode), lambda: build_mlp(mlp_mode))
        ims = []
        for c in range(8):
            hh = c % 2
            m = pro_inputs(c, f(norm_mlp_g[i]), mlp_mode)
            m["w1"] = _c(f(mlp_w1[i])[:, hh * 2048:(hh + 1) * 2048])
            m["w2"] = _c(f(mlp_w2[i])[hh * 2048:(hh + 1) * 2048, :])
            ims.append(m)
        r = _launch(nc, ims)
        xs = [_c(r[c]["xo"]) for c in range(8)]
        part = [_c(r[c]["po"]) for c in range(8)]
    nc = _prog(("final",), build_final)
    gb = _c(np.broadcast_to(f(final_norm_g)[None, :], (128, D)))
    ims = []
    for c in range(8):
        hh = c % 2
        rows = slice(hh * 2048, (hh + 1) * 2048)
        ims.append({"x": _c(xs[c][rows]), "pa": _c(part[c][rows]), "pb": _c(part[c ^ 1][rows]), "gb": gb})
    r = _launch(nc, ims)
    out = np.zeros((4, S_LEN, D), np.float32)
    for c in range(8):
        hh = c % 2
        out[c // 2, hh * 2048:(hh + 1) * 2048] = r[c]["out"]
    return out
```
